# Optimizing a Trainium2 kernel written in Bass

```python
import math
import jax, jax.numpy as jnp
from jax import lax
import numpy as np

D_MODEL = 1024
BATCH = 8
SEQ = 2048
DEPTH = 1
DEC_BATCH = 16
DEC_SEQ = 2048
PAST_LEN = 128

N_META = 16
SSM_HEADS = 16
SSM_HEAD_DIM = 64
D_SSM = SSM_HEADS * SSM_HEAD_DIM
SSM_GROUPS = 2
HEADS_PER_GROUP = SSM_HEADS // SSM_GROUPS
D_STATE = 128
D_CONV = 5
D_CONV_CH = D_SSM + 2 * SSM_GROUPS * D_STATE
CHUNK = 128
META_PAD = CHUNK - N_META
MLA_HEADS = 8
QK_NOPE_DIM = 64
QK_ROPE_DIM = 32
V_HEAD_DIM = 64
Q_LORA_RANK = 384
KV_LORA_RANK = 256
D_ATTN = MLA_HEADS * V_HEAD_DIM
ROPE_THETA = 10000.0
Q_BLOCK = 128
D_MIX = D_SSM + D_ATTN
SPLIT_Z = D_SSM
SPLIT_XBC = SPLIT_Z + D_CONV_CH
SPLIT_DT = SPLIT_XBC + 2 * SSM_HEADS
SPLIT_CQ = SPLIT_DT + Q_LORA_RANK
SPLIT_CKV = SPLIT_CQ + KV_LORA_RANK
D_IN_PROJ = SPLIT_CKV + QK_ROPE_DIM
PEER_HEADS = 8
N_KEYS = 128
N_EXPERTS = N_KEYS * N_KEYS
PEER_TOPK = 16
D_KEY = 256
D_SUBKEY = D_KEY // 2
PEER_BLOCK = 256
DEEPNORM_ALPHA = (2.0 * DEPTH) ** 0.25
DEEPNORM_BETA = (8.0 * DEPTH) ** -0.25
EPS = 1e-5

kernel_name = "hymba_ssd_mla_peer_encoder"


def layer_norm(x, g, b):
    xf = x.astype(jnp.float32)
    mu = jnp.mean(xf, -1, keepdims=True)
    var = jnp.mean(jnp.square(xf - mu), -1, keepdims=True)
    return ((xf - mu) * lax.rsqrt(var + EPS) * g.astype(jnp.float32) + b.astype(jnp.float32)).astype(x.dtype)


def rms_norm(x, g):
    xf = x.astype(jnp.float32)
    ms = jnp.mean(jnp.square(xf), -1, keepdims=True)
    return (xf * lax.rsqrt(ms + EPS) * g.astype(jnp.float32)).astype(x.dtype)


def depthwise_conv(x, w, bias):
    c = x.shape[-1]
    y = lax.conv_general_dilated(x, w[:, None, :].astype(x.dtype), window_strides=(1,),
                                 padding=[(D_CONV // 2, D_CONV // 2)],
                                 dimension_numbers=("NWC", "WIO", "NWC"), feature_group_count=c)
    return y + bias.astype(x.dtype)


def segsum(a):
    t = a.shape[-1]
    cs = jnp.cumsum(a, -1)
    d = cs[..., :, None] - cs[..., None, :]
    mask = jnp.tril(jnp.ones((t, t), dtype=bool))
    return jnp.where(mask, d, -jnp.inf)


def ssd_chunked(xdt, da, bm, cm):
    b, lp = xdt.shape[:2]
    nc = lp // CHUNK
    x = xdt.reshape(b, nc, CHUNK, SSM_GROUPS, HEADS_PER_GROUP, SSM_HEAD_DIM)
    bc = bm.reshape(b, nc, CHUNK, SSM_GROUPS, D_STATE)
    cc = cm.reshape(b, nc, CHUNK, SSM_GROUPS, D_STATE)
    a = da.reshape(b, nc, CHUNK, SSM_GROUPS, HEADS_PER_GROUP).transpose(0, 3, 4, 1, 2)
    a_cs = jnp.cumsum(a, -1)
    lmat = jnp.exp(segsum(a))
    cb = jnp.einsum("bclgn,bcsgn->bcgls", cc, bc)
    y_diag = jnp.einsum("bcgls,bghcls,bcsghp->bclghp", cb, lmat, x)
    decay_states = jnp.exp(a_cs[..., -1:] - a_cs)
    states = jnp.einsum("bclgn,bghcl,bclghp->bcghpn", bc, decay_states, x)
    states = jnp.concatenate([jnp.zeros_like(states[:, :1]), states], axis=1)
    a_last = jnp.pad(a_cs[..., -1], ((0, 0), (0, 0), (0, 0), (1, 0)))
    chunk_decay = jnp.exp(segsum(a_last))
    new_states = jnp.einsum("bghzc,bcghpn->bzghpn", chunk_decay, states)
    prev_states = new_states[:, :-1]
    y_off = jnp.einsum("bclgn,bcghpn,bghcl->bclghp", cc, prev_states, jnp.exp(a_cs))
    return (y_diag + y_off).reshape(b, lp, SSM_GROUPS, HEADS_PER_GROUP, SSM_HEAD_DIM)


def rope_tables(pos):
    half = QK_ROPE_DIM // 2
    inv = ROPE_THETA ** (-jnp.arange(half, dtype=jnp.float32) / half)
    ang = pos[:, None] * inv[None, :]
    return jnp.cos(ang), jnp.sin(ang)


def apply_rope(x, cos, sin):
    x1, x2 = jnp.split(x, 2, axis=-1)
    return jnp.concatenate([x1 * cos - x2 * sin, x1 * sin + x2 * cos], -1).astype(x.dtype)


def block_attention(q_nope, q_rope, k_nope, k_rope, v):
    b, l, h, _ = q_nope.shape
    nb = -(-l // Q_BLOCK)
    pad = nb * Q_BLOCK - l

    def to_blocks(t):
        t = jnp.pad(t, ((0, 0), (0, pad), (0, 0), (0, 0)))
        return t.reshape(b, nb, Q_BLOCK, h, t.shape[-1]).transpose(1, 0, 2, 3, 4)

    scale = 1.0 / math.sqrt(QK_NOPE_DIM + QK_ROPE_DIM)

    def one_block(qb):
        qn, qr = qb
        s = jnp.einsum("bqhd,bkhd->bhqk", qn, k_nope) + jnp.einsum("bqhr,bkr->bhqk", qr, k_rope)
        pr = jax.nn.softmax(s.astype(jnp.float32) * scale, axis=-1)
        return jnp.einsum("bhqk,bkhd->bqhd", pr.astype(v.dtype), v)

    o = lax.map(one_block, (to_blocks(q_nope), to_blocks(q_rope)))
    return o.transpose(1, 0, 2, 3, 4).reshape(b, nb * Q_BLOCK, h, V_HEAD_DIM)[:, :l]


def pad_front(t):
    return jnp.pad(t, [(0, 0), (META_PAD, 0)] + [(0, 0)] * (t.ndim - 2))


def hybrid_mixer(h, pos, p):
    b, l, _ = h.shape
    proj = h @ p["w_in"]
    z = proj[..., :SPLIT_Z]
    xbc = proj[..., SPLIT_Z:SPLIT_XBC]
    dt_raw = proj[..., SPLIT_XBC:SPLIT_DT].astype(jnp.float32)
    c_q = proj[..., SPLIT_DT:SPLIT_CQ]
    c_kv = proj[..., SPLIT_CQ:SPLIT_CKV]
    k_rope = proj[..., SPLIT_CKV:]

    xbc = jax.nn.silu(depthwise_conv(xbc, p["conv_w"], p["conv_b"]))
    xs = xbc[..., :D_SSM].reshape(b, l, SSM_GROUPS, HEADS_PER_GROUP, SSM_HEAD_DIM)
    bm = xbc[..., D_SSM:D_SSM + SSM_GROUPS * D_STATE].reshape(b, l, SSM_GROUPS, D_STATE)
    cm = xbc[..., D_SSM + SSM_GROUPS * D_STATE:].reshape(b, l, SSM_GROUPS, D_STATE)
    gh = (SSM_GROUPS, HEADS_PER_GROUP)
    dt_f = jax.nn.softplus(dt_raw[..., :SSM_HEADS] + p["dt_bias_fwd"].astype(jnp.float32)).reshape(b, l, *gh)
    dt_b = jax.nn.softplus(dt_raw[..., SSM_HEADS:] + p["dt_bias_bwd"].astype(jnp.float32)).reshape(b, l, *gh)
    a_f = -jnp.exp(p["a_log_fwd"].astype(jnp.float32)).reshape(gh)
    a_b = -jnp.exp(p["a_log_bwd"].astype(jnp.float32)).reshape(gh)
    xs_p, bm_p, cm_p = pad_front(xs), pad_front(bm), pad_front(cm)
    dt_f_p, dt_b_p = pad_front(dt_f), pad_front(dt_b)
    y_f = ssd_chunked(xs_p * dt_f_p[..., None], dt_f_p * a_f, bm_p, cm_p)
    flip = lambda t: jnp.flip(t, axis=1)
    y_b = flip(ssd_chunked(flip(xs_p * dt_b_p[..., None]), flip(dt_b_p * a_b), flip(bm_p), flip(cm_p)))
    y = (y_f + y_b)[:, META_PAD:] + p["d_skip"].astype(jnp.float32).reshape(*gh, 1) * xs
    y = y.reshape(b, l, D_SSM).astype(h.dtype)
    y_ssm = rms_norm(y * jax.nn.silu(z), p["ssm_norm_g"])

    q = (rms_norm(c_q, p["q_norm_g"]) @ p["w_uq"]).reshape(b, l, MLA_HEADS, QK_NOPE_DIM + QK_ROPE_DIM)
    kv = (rms_norm(c_kv, p["kv_norm_g"]) @ p["w_ukv"]).reshape(b, l, MLA_HEADS, QK_NOPE_DIM + V_HEAD_DIM)
    cos, sin = rope_tables(pos)
    q_nope = q[..., :QK_NOPE_DIM]
    q_rope = apply_rope(q[..., QK_NOPE_DIM:], cos[:, None, :], sin[:, None, :])
    k_nope = kv[..., :QK_NOPE_DIM]
    v = kv[..., QK_NOPE_DIM:]
    k_rope = apply_rope(k_rope, cos, sin)
    o = block_attention(q_nope, q_rope, k_nope, k_rope, v).reshape(b, l, D_ATTN)
    y_attn = rms_norm(o.astype(h.dtype), p["attn_norm_g"])

    return jnp.concatenate([y_ssm, y_attn], axis=-1) @ p["w_out"]


def peer_ffn(h, p):
    b, l, d = h.shape
    t = b * l
    nblk = -(-t // PEER_BLOCK)
    xt = jnp.pad(h.reshape(t, d), ((0, nblk * PEER_BLOCK - t), (0, 0))).reshape(nblk, PEER_BLOCK, d)
    w_q, sub_keys, u_tab, v_tab = p["peer_w_query"], p["peer_sub_keys"], p["peer_u"], p["peer_v"]

    def block(xb):
        q = (xb @ w_q).reshape(PEER_BLOCK, PEER_HEADS, 2, D_SUBKEY)
        s1 = jnp.einsum("thd,hnd->thn", q[:, :, 0], sub_keys[0]).astype(jnp.float32)
        s2 = jnp.einsum("thd,hnd->thn", q[:, :, 1], sub_keys[1]).astype(jnp.float32)
        v1, i1 = lax.top_k(s1, PEER_TOPK)
        v2, i2 = lax.top_k(s2, PEER_TOPK)
        cand = (v1[..., :, None] + v2[..., None, :]).reshape(PEER_BLOCK, PEER_HEADS, PEER_TOPK * PEER_TOPK)
        cidx = (i1[..., :, None] * N_KEYS + i2[..., None, :]).reshape(PEER_BLOCK, PEER_HEADS, PEER_TOPK * PEER_TOPK)
        top_s, sel = lax.top_k(cand, PEER_TOPK)
        experts = jnp.take_along_axis(cidx, sel, axis=-1)
        g = jax.nn.softmax(top_s, axis=-1)
        u = jnp.take(u_tab, experts, axis=0)
        act = jax.nn.gelu(jnp.einsum("thkd,td->thk", u, xb).astype(jnp.float32), approximate=False)
        ve = jnp.take(v_tab, experts, axis=0)
        return jnp.einsum("thk,thkd->td", (g * act).astype(ve.dtype), ve)

    out = lax.map(block, xt).reshape(nblk * PEER_BLOCK, d)[:t]
    return out.reshape(b, l, d).astype(h.dtype)


def encode(x, meta_tokens, ln_in_g, ln_in_b, layer_params):
    b, s, _ = x.shape
    meta = jnp.broadcast_to(meta_tokens.astype(x.dtype)[None], (b, N_META, D_MODEL))
    h = layer_norm(jnp.concatenate([meta, x], axis=1), ln_in_g, ln_in_b)
    pos = jnp.arange(s + N_META, dtype=jnp.float32)
    for li in range(DEPTH):
        p = {k: w[li] for k, w in layer_params.items()}
        h = layer_norm(DEEPNORM_ALPHA * h + hybrid_mixer(h, pos, p), p["ln1_g"], p["ln1_b"])
        h = layer_norm(DEEPNORM_ALPHA * h + peer_ffn(h, p), p["ln2_g"], p["ln2_b"])
    return h[:, N_META:]


def setup_inputs(seed: int = 0) -> dict:
    key = jax.random.key(seed)
    ks = jax.random.split(key, 32)
    nrm = lambda k, shape, s: jax.random.normal(k, shape, jnp.float32) * s
    gain = lambda k, n: 1.0 + 0.02 * jax.random.normal(k, (DEPTH, n), jnp.float32)
    bias = lambda k, n: 0.02 * jax.random.normal(k, (DEPTH, n), jnp.float32)

    def dt_bias(k):
        dt = jnp.exp(jax.random.uniform(k, (DEPTH, SSM_HEADS), jnp.float32, math.log(1e-3), math.log(1e-1)))
        return dt + jnp.log(-jnp.expm1(-dt))

    return {
        "x_prompt": jax.random.normal(ks[0], (BATCH, SEQ, D_MODEL), jnp.float32),
        "x_sample": jax.random.normal(ks[1], (DEC_BATCH, DEC_SEQ, D_MODEL), jnp.float32),
        "meta_tokens": nrm(ks[2], (N_META, D_MODEL), 1.0),
        "ln_in_g": 1.0 + 0.02 * jax.random.normal(ks[3], (D_MODEL,), jnp.float32),
        "ln_in_b": 0.02 * jax.random.normal(ks[4], (D_MODEL,), jnp.float32),
        "w_in": nrm(ks[5], (DEPTH, D_MODEL, D_IN_PROJ), D_MODEL ** -0.5),
        "conv_w": nrm(ks[6], (DEPTH, D_CONV, D_CONV_CH), D_CONV ** -0.5),
        "conv_b": bias(ks[7], D_CONV_CH),
        "dt_bias_fwd": dt_bias(ks[8]),
        "dt_bias_bwd": dt_bias(ks[9]),
        "a_log_fwd": jnp.log(jax.random.uniform(ks[10], (DEPTH, SSM_HEADS), jnp.float32, 1.0, 16.0)),
        "a_log_bwd": jnp.log(jax.random.uniform(ks[11], (DEPTH, SSM_HEADS), jnp.float32, 1.0, 16.0)),
        "d_skip": gain(ks[12], SSM_HEADS),
        "ssm_norm_g": gain(ks[13], D_SSM),
        "q_norm_g": gain(ks[14], Q_LORA_RANK),
        "w_uq": nrm(ks[15], (DEPTH, Q_LORA_RANK, MLA_HEADS * (QK_NOPE_DIM + QK_ROPE_DIM)), Q_LORA_RANK ** -0.5),
        "kv_norm_g": gain(ks[16], KV_LORA_RANK),
        "w_ukv": nrm(ks[17], (DEPTH, KV_LORA_RANK, MLA_HEADS * (QK_NOPE_DIM + V_HEAD_DIM)), KV_LORA_RANK ** -0.5),
        "attn_norm_g": gain(ks[18], D_ATTN),
        "w_out": nrm(ks[19], (DEPTH, D_MIX, D_MODEL), DEEPNORM_BETA * D_MIX ** -0.5),
        "ln1_g": gain(ks[20], D_MODEL),
        "ln1_b": bias(ks[21], D_MODEL),
        "peer_w_query": nrm(ks[22], (DEPTH, D_MODEL, PEER_HEADS * D_KEY), D_MODEL ** -0.5),
        "peer_sub_keys": nrm(ks[23], (DEPTH, 2, PEER_HEADS, N_KEYS, D_SUBKEY), D_SUBKEY ** -0.5),
        "peer_u": nrm(ks[24], (DEPTH, N_EXPERTS, D_MODEL), D_MODEL ** -0.5),
        "peer_v": nrm(ks[25], (DEPTH, N_EXPERTS, D_MODEL), DEEPNORM_BETA * PEER_HEADS ** -0.5),
        "ln2_g": gain(ks[26], D_MODEL),
        "ln2_b": bias(ks[27], D_MODEL),
    }


def reference(x_prompt, x_sample, meta_tokens, ln_in_g, ln_in_b, w_in, conv_w, conv_b,
              dt_bias_fwd, dt_bias_bwd, a_log_fwd, a_log_bwd, d_skip, ssm_norm_g,
              q_norm_g, w_uq, kv_norm_g, w_ukv, attn_norm_g, w_out, ln1_g, ln1_b,
              peer_w_query, peer_sub_keys, peer_u, peer_v, ln2_g, ln2_b):
    layer_params = dict(w_in=w_in, conv_w=conv_w, conv_b=conv_b, dt_bias_fwd=dt_bias_fwd,
                        dt_bias_bwd=dt_bias_bwd, a_log_fwd=a_log_fwd, a_log_bwd=a_log_bwd,
                        d_skip=d_skip, ssm_norm_g=ssm_norm_g, q_norm_g=q_norm_g, w_uq=w_uq,
                        kv_norm_g=kv_norm_g, w_ukv=w_ukv, attn_norm_g=attn_norm_g, w_out=w_out,
                        ln1_g=ln1_g, ln1_b=ln1_b, peer_w_query=peer_w_query,
                        peer_sub_keys=peer_sub_keys, peer_u=peer_u, peer_v=peer_v,
                        ln2_g=ln2_g, ln2_b=ln2_b)
    y_prompt = encode(x_prompt, meta_tokens, ln_in_g, ln_in_b, layer_params)
    y_sample = encode(x_sample, meta_tokens, ln_in_g, ln_in_b, layer_params)
    return (y_prompt, y_sample)
```

```python
import numpy as np
import concourse.bass as bass
import concourse.mybir as mybir
from concourse.bass_utils import run_bass_kernel_spmd
from contextlib import ExitStack

F32 = mybir.dt.float32
BF16 = mybir.dt.bfloat16
U32 = mybir.dt.uint32
I32 = mybir.dt.int32
AF = mybir.ActivationFunctionType
ALU = mybir.AluOpType
AX = mybir.AxisListType

SAME_ENGINE_SYNC = True


class Reg:
    __slots__ = ("name", "w", "r")

    def __init__(self, name):
        self.name = name
        self.w = []
        self.r = []


class V:
    __slots__ = ("ap", "regs")

    def __init__(self, ap, regs):
        self.ap = ap
        self.regs = regs

    def __getitem__(self, key):
        return V(self.ap[key], self.regs)

    def re(self, s, **kw):
        return V(self.ap.rearrange(s, **kw), self.regs)

    def bc(self, axis, shape):
        return V(self.ap.unsqueeze(axis).to_broadcast(list(shape)), self.regs)

    def bcast(self, shape):
        return V(self.ap.to_broadcast(list(shape)), self.regs)

    def bitcast(self, dt):
        return V(self.ap.bitcast(dt), self.regs)


class T:
    def __init__(self, handle, name, nreg=1):
        self.h = handle
        self.name = name
        self.regs = [Reg(f"{name}.{i}") for i in range(nreg)]

    def __getitem__(self, key):
        return V(self.h[key], self.regs)

    def s(self, i, key=None):
        ap = None if self.h is None else (self.h[key] if key is not None else self.h[:])
        return V(ap, [self.regs[i]])

    def ss(self, idxs, key=None):
        ap = None if self.h is None else (self.h[key] if key is not None else self.h[:])
        return V(ap, [self.regs[i] for i in idxs])


ENG = ["pe", "act", "dve", "pool", "sp"]
BLOCKNAME = dict(pe="tensor", act="scalar", dve="vector", pool="gpsimd", sp="sync")


class KB:
    def __init__(self, nc):
        self.nc = nc
        self.engs = dict(pe=nc.tensor, act=nc.scalar, dve=nc.vector, pool=nc.gpsimd, sp=nc.sync)
        self.sem = {k: nc.alloc_semaphore("s_" + k) for k in ENG}
        self.cnt = {k: 0 for k in ENG}
        self.known = {k: {} for k in ENG}
        self.stream = {k: [] for k in ENG}
        self.dcnt = {}
        self.stack = ExitStack()
        self.nops = 0

    def sb(self, name, shape, dtype, nreg=1, stack=None):
        self.uid = getattr(self, "uid", 0) + 1
        h = (stack or self.stack).enter_context(self.nc.sbuf_tensor(f"sb{self.uid}_{name}", list(shape), dtype))
        return T(h, name, nreg)

    def psum(self, name, shape, dtype=F32, nreg=1, stack=None):
        self.uid = getattr(self, "uid", 0) + 1
        h = (stack or self.stack).enter_context(self.nc.psum_tensor(f"pp{self.uid}_{name}", list(shape), dtype))
        return T(h, name, nreg)

    def dsem(self, name):
        if name not in self.dcnt:
            self.sem[name] = self.nc.alloc_semaphore("d_" + name)
            self.dcnt[name] = 0
        return name

    def _deps(self, eng, reads, writes):
        deps = []
        for v in reads:
            for r in v.regs:
                deps += r.w
        for v in writes:
            for r in v.regs:
                deps += r.w
                deps += r.r
        need = {}
        for (sk, val) in deps:
            if sk == eng and (eng == "pe" or eng == "sp" or not SAME_ENGINE_SYNC):
                continue
            if self.known[eng].get(sk, 0) >= val:
                continue
            if need.get(sk, 0) < val:
                need[sk] = val
        for sk, val in need.items():
            self.stream[eng].append(("w", sk, val))
            self.known[eng][sk] = val

    def _commit(self, ev, reads, writes):
        for v in writes:
            for r in v.regs:
                r.w = [ev]
                r.r = []
        for v in reads:
            for r in v.regs:
                r.r = [e for e in r.r if e[0] != ev[0]]
                r.r.append(ev)

    def op(self, eng, fn, reads=(), writes=()):
        self._deps(eng, reads, writes)
        self.cnt[eng] += 1
        ev = (eng, self.cnt[eng])
        self.stream[eng].append(("o", fn))
        self._commit(ev, reads, writes)
        self.nops += 1

    def dma(self, q, out, in_, reads=(), writes=(), sem=None, **kw):
        reads = list(reads)
        writes = list(writes)
        if isinstance(out, V):
            writes.append(out)
            out = out.ap
        if isinstance(in_, V):
            reads.append(in_)
            in_ = in_.ap
        if sem is None:
            sem = "L_" + writes[0].regs[0].name.split(".")[0]
        self.dsem(sem)
        self._deps(q, reads, writes)
        if "throttle" in kw:
            lag = kw.pop("throttle")
            tv = self.dcnt[sem] - 16 * lag
            if tv > 0 and self.known[q].get(sem, 0) < tv:
                self.stream[q].append(("w", sem, tv))
                self.known[q][sem] = tv
        self.dcnt[sem] += 16
        ev = (sem, self.dcnt[sem])
        self.stream[q].append(("d", out, in_, sem, kw))
        self._commit(ev, reads, writes)
        self.nops += 1

    def flush(self, label=None):
        self.marks = getattr(self, "marks", [])
        self.marks.append((label, dict(self.cnt)))
        for e in ENG:
            for k in ENG:
                if k != e and k != "sp" and self.known[e].get(k, 0) < self.cnt[k]:
                    self.stream[e].append(("w", k, self.cnt[k]))
                    self.known[e][k] = self.cnt[k]
            for k, c in self.dcnt.items():
                if self.known[e].get(k, 0) < c:
                    self.stream[e].append(("w", k, c))
                    self.known[e][k] = c
        nc = self.nc
        with nc.Block() as block:
            for e in ENG:
                items = self.stream[e]
                engine = self.engs[e]
                semh = self.sem[e]

                def body(_e, items=items, engine=engine, semh=semh):
                    for it in items:
                        if it[0] == "w":
                            engine.wait_ge(self.sem[it[1]], it[2])
                        elif it[0] == "o":
                            it[1](engine).then_inc(semh, 1)
                        else:
                            _, out, in_, sem, kw = it
                            engine.dma_start(out=out, in_=in_, **kw).then_inc(self.sem[sem], 16)

                getattr(block, BLOCKNAME[e])(body)
        self.stream = {k: [] for k in ENG}

    def mm(self, out, lhsT, rhs, start=True, stop=True):
        self.op("pe", lambda e: e.matmul(out.ap, lhsT.ap, rhs.ap, start=start, stop=stop),
                reads=[lhsT, rhs], writes=[out])

    def tr(self, out, in_, ident):
        self.op("pe", lambda e: e.transpose(out.ap, in_.ap, ident.ap), reads=[in_, ident], writes=[out])

    def act(self, out, in_, func, bias=None, scale=None, accum=None, eng="act"):
        kw = {}
        reads = [in_]
        writes = [out]
        if bias is not None:
            if isinstance(bias, V):
                reads.append(bias)
                kw["bias"] = bias.ap
            else:
                kw["bias"] = bias
        if scale is not None:
            if isinstance(scale, V):
                reads.append(scale)
                kw["scale"] = scale.ap
            else:
                kw["scale"] = scale
        if accum is not None:
            writes.append(accum)
            kw["accum_out"] = accum.ap
        self.op("act", lambda e: e.activation(out=out.ap, in_=in_.ap, func=func, **kw), reads=reads, writes=writes)

    def tt(self, eng, out, in0, in1, op):
        self.op(eng, lambda e: e.tensor_tensor(out=out.ap, in0=in0.ap, in1=in1.ap, op=op),
                reads=[in0, in1], writes=[out])

    def ts(self, eng, out, in0, s1, s2=None, op0=ALU.mult, op1=None, accum=None):
        reads = [in0]
        writes = [out]
        a1 = s1
        a2 = s2
        if isinstance(s1, V):
            reads.append(s1)
            a1 = s1.ap
        if isinstance(s2, V):
            reads.append(s2)
            a2 = s2.ap
        kw = {}
        if op1 is not None:
            kw["op1"] = op1
        if accum is not None:
            writes.append(accum)
            kw["accum_out"] = accum.ap
        self.op(eng, lambda e: e.tensor_scalar(out=out.ap, in0=in0.ap, scalar1=a1, scalar2=a2, op0=op0, **kw),
                reads=reads, writes=writes)

    def stt(self, out, in0, scalar, in1, op0, op1, eng="dve"):
        reads = [in0, in1]
        a = scalar
        if isinstance(scalar, V):
            reads.append(scalar)
            a = scalar.ap
        self.op(eng, lambda e: e.scalar_tensor_tensor(out=out.ap, in0=in0.ap, scalar=a, in1=in1.ap, op0=op0, op1=op1),
                reads=reads, writes=[out])

    def copy(self, eng, out, in_):
        if eng == "act":
            self.op("act", lambda e: e.copy(out=out.ap, in_=in_.ap), reads=[in_], writes=[out])
        else:
            self.op(eng, lambda e: e.tensor_copy(out=out.ap, in_=in_.ap), reads=[in_], writes=[out])

    def memset(self, eng, out, val):
        self.op(eng, lambda e: e.memset(out.ap, val), writes=[out])

    def reduce(self, out, in_, op, axis=AX.X, eng="dve"):
        self.op(eng, lambda e: e.tensor_reduce(out=out.ap, in_=in_.ap, axis=axis, op=op), reads=[in_], writes=[out])

    def recip(self, out, in_, eng="dve"):
        self.op(eng, lambda e: e.reciprocal(out=out.ap, in_=in_.ap), reads=[in_], writes=[out])


D = 1024
N_META = 16
NH_SSM = 16
HD = 64
D_SSM = 1024
NG = 2
D_STATE = 128
D_CONV = 5
D_CONV_CH = 1536
MLA_H = 8
DN = 64
DR = 32
DV = 64
QLORA = 384
KVLORA = 256
D_ATTN = 512
D_MIX = 1536
SPLIT_Z = 1024
SPLIT_XBC = 2560
SPLIT_DT = 2592
SPLIT_CQ = 2976
SPLIT_CKV = 3232
D_IN = 3264
PEER_H = 8
NKEYS = 128
TOPK = 16
ALPHA = 2.0 ** 0.25
EPS = 1e-5
ATT_SCALE = 1.0 / (96.0 ** 0.5)


def build(S, NSEQ, dbg=()):
    NT = S // 128
    NTT = NT + 1
    LP = NTT * 128
    QG = min(512, S)
    NQG = S // QG
    nc = bass.Bass("TRN2", target_bir_lowering=False)
    K = KB(nc)

    def din(name, shape, dt=F32):
        return nc.dram_tensor(name, list(shape), dt, kind="ExternalInput").ap()

    x_d = din("x", [NSEQ, S, D])
    y_d = nc.dram_tensor("y", [NSEQ, S, D], F32, kind="ExternalOutput").ap()
    meta_d = din("meta", [N_META, D])
    vecs_d = din("vecs", [8, D])
    w_in_d = din("w_in", [D, D_IN])
    w_uq_d = din("w_uq", [QLORA, 768])
    w_ukv_d = din("w_ukv", [KVLORA, 1024])
    qg_d = din("q_norm_g", [128, 3])
    kvg_d = din("kv_norm_g", [128, 2])
    cs_d = din("cossin", [LP, 32])
    ident_d = din("ident", [128, 128])
    pad_d = din("padmask", [128, 1])
    cw_d = din("conv_w", [128, 12, 5])
    cb_d = din("conv_b", [128, 12])
    dtb_d = din("dt_bias", [1, 32])
    alog_d = din("a_log", [1, 32])
    dsk_d = din("d_skip", [1, 16])
    le_d = din("mask_le", [128, 128])
    ge_d = din("mask_ge", [128, 128])
    w_out_d = din("w_out", [D_MIX, D])
    ng_d = din("norm_g", [128, 12])
    wq_d = din("peer_wq", [D, 2048])
    skT_d = din("peer_skT", [128, 16, 128])
    uT_d = din("peer_uT", [128 * 128, 1024])
    v_d = din("peer_v", [128 * 128, 1024])
    iota_d = din("iota", [128, 128])
    uT_s = nc.dram_tensor("uT_s", [128 * 128, 1024], BF16, kind="Internal").ap()
    v_s = nc.dram_tensor("v_s", [128 * 128, 1024], BF16, kind="Internal").ap()
    uv_r = T(None, "uv_s", 2)
    xs_s = nc.dram_tensor("xs_s", [NTT, 128, 1024], BF16, kind="Internal").ap()
    sz_s = nc.dram_tensor("sz_s", [NT, 128, 1024], BF16, kind="Internal").ap()
    sinb_s = nc.dram_tensor("sinb_s", [NTT, 128, 1024], BF16, kind="Internal").ap()
    h1_s = nc.dram_tensor("h1_s", [NSEQ, S, D], F32, kind="Internal").ap()
    xs_r = T(None, "xs_s", NTT)
    sz_r = T(None, "sz_s", NT)
    sinb_r = T(None, "sinb_s", NTT)
    h1_r = T(None, "h1_s", NSEQ * NT)
    dbg_out = {}

    def dump(name, v, shape, dt=F32):
        if name not in dbg:
            return
        d = nc.dram_tensor("dbg_" + name, list(shape), dt, kind="ExternalOutput").ap()
        dbg_out[name] = d
        K.dma("sp", d, v, sem="dbg")

    st = K.stack
    with st:
        ident = K.sb("ident", [128, 128], F32)
        K.dma("sp", ident[:], ident_d)
        padm = K.sb("padm", [128, 1], F32)
        K.dma("sp", padm[:], pad_d)
        cast_state = [0]
        cst_box = [None]

        def issue_casts(n):
            if cst_box[0] is None:
                return
            for _ in range(n):
                i = cast_state[0]
                if i >= 64:
                    return
                cast_state[0] += 1
                r0 = (i // 2) * 512
                stg_ = cst_box[0][i % 2]
                src, dst, reg = (uT_d, uT_s, uv_r.s(0)) if i % 2 == 0 else (v_d, v_s, uv_r.s(1))
                K.dma("pool", stg_[:], src[r0:r0 + 512, :].rearrange("(j p) c -> p j c", p=128), sem="cst_l" + str(i % 2))
                K.dma("sp", dst[r0:r0 + 512, :].rearrange("(j p) c -> p j c", p=128), stg_[:], writes=[reg], sem="cst_s" + str(i % 2))
        ones = K.sb("ones", [128, 128], F32)
        K.memset("pool", ones[:], 1.0)
        cs_all = K.sb("cs_all", [128, NTT, 32], F32)
        K.dma("sp", cs_all[:], cs_d.rearrange("(t p) c -> p t c", p=128))

        def alloc_ln(stack, i0, nb=2):
            vecs_box[0] = K.sb("vecs", [128, 2, D], F32, stack=stack)
            for i in range(2):
                K.dma("sp", vecs_box[0][:, i, :], vecs_d[i0 + i:i0 + i + 1, :].to_broadcast([128, D]))
            for i in range(nb):
                xb[i] = K.sb(f"xb{i}", [128, D], F32, stack=stack)
                xn[i] = K.sb(f"xn{i}", [128, D], F32, stack=stack)
        PS = [K.psum(f"ps{i}", [128, 1024], F32, nreg=2) for i in range(4)]

        def bank(i, b, n=512):
            return PS[i].s(b, (slice(None), slice(b * 512, b * 512 + n)))
        mle = K.sb("mle", [128, 128], F32)
        mge = K.sb("mge", [128, 128], F32)
        K.dma("sp", mle[:], le_d)
        K.dma("sp", mge[:], ge_d)
        mgt = K.sb("mgt", [128, 128], BF16)
        mlt = K.sb("mlt", [128, 128], BF16)
        K.ts("dve", mgt[:], mle[:], -1.0, 1.0, op0=ALU.mult, op1=ALU.add)
        K.ts("dve", mlt[:], mge[:], -1.0, 1.0, op0=ALU.mult, op1=ALU.add)
        dsk = K.sb("dsk", [128, 16], F32)
        K.dma("sp", dsk[:], dsk_d.to_broadcast([128, 16]))
        xb = [None, None]
        xn = [None, None]
        vecs_box = [None]
        stt = K.sb("stt", [128, 2, 6], F32)
        mv = K.sb("mv", [128, 2], F32)
        rs = K.sb("rs", [128, 1], F32)

        def ln_tile(src, dst, gi, bi):
            K.op("dve", lambda e: e.bn_stats(out=stt.h[:, 0, :], in_=src.ap[:, 0:512]), reads=[src], writes=[stt[:]])
            K.op("dve", lambda e: e.bn_stats(out=stt.h[:, 1, :], in_=src.ap[:, 512:1024]), reads=[src], writes=[stt[:]])
            K.op("dve", lambda e: e.bn_aggr(out=mv.h[:], in_=stt.h[:].rearrange("p a b -> p (a b)")), reads=[stt[:]], writes=[mv[:]])
            K.ts("dve", rs[:], mv[:, 1:2], EPS, None, op0=ALU.add)
            K.act(rs[:], rs[:], AF.Sqrt)
            K.recip(rs[:], rs[:])
            K.ts("dve", dst, src, mv[:, 0:1], rs[:], op0=ALU.subtract, op1=ALU.mult)
            K.tt("dve", dst, dst, vecs_box[0][:, gi, :], ALU.mult)
            K.tt("pool", dst, dst, vecs_box[0][:, bi, :], ALU.add)

        def load_x(seq, t, buf):
            if t == 0:
                K.memset("pool", buf[:], 0.0)
                K.dma("sp", buf[112:128, :], meta_d, sem="x" + buf.name)
            else:
                K.dma("sp", buf[:], x_d[seq, (t - 1) * 128:t * 128, :], sem="x" + buf.name)

        for seq in range(NSEQ):
            with ExitStack() as s_seq:
                if seq == 0:
                    cst_box[0] = [K.sb(f"cst{i}", [128, 4, 1024], BF16, stack=s_seq) for i in range(2)]
                yaT = K.sb("yaT", [128, 4, S], BF16, stack=s_seq)
                hT = K.sb("hT", [128, 8, LP], BF16, stack=s_seq)
                s1 = ExitStack()
                alloc_ln(s1, 0)
                for t in range(NTT):
                    buf = xb[t % 2]
                    dst = xn[t % 2]
                    load_x(seq, t, buf)
                    ln_tile(buf[:], dst[:], 0, 1)
                    for half in range(2):
                        pb = PS[half]
                        for j in range(4):
                            kc = half * 4 + j
                            K.tr(pb[:, j * 128:(j + 1) * 128], dst[:, kc * 128:(kc + 1) * 128], ident[:])
                        K.copy("act", hT[:, half * 4:half * 4 + 4, t * 128:(t + 1) * 128],
                               pb[:, 0:512].re("p (a b) -> p a b", a=4))
                if seq == 0:
                    dump("hT", hT[:], [128, 8, LP], BF16)
                K.flush("p1_ln")
                s1.close()
                with ExitStack() as s2:
                    wm = K.sb("wm", [128, 8, 672], BF16, stack=s2)
                    K.dma("pool", wm[:], w_in_d[:, SPLIT_DT:D_IN].rearrange("(kc p) c -> p kc c", p=128))
                    wuq = K.sb("wuq", [128, 3, 768], BF16, stack=s2)
                    wukv = K.sb("wukv", [128, 2, 1024], BF16, stack=s2)
                    with ExitStack() as s_tmp:
                        wf = K.sb("wf", [128, 3, 768], F32, stack=s_tmp)
                        wf2 = K.sb("wf2", [128, 2, 1024], F32, stack=s_tmp)
                        gq = K.sb("gq", [128, 3], F32, stack=s_tmp)
                        gkv = K.sb("gkv", [128, 2], F32, stack=s_tmp)
                        K.dma("sp", wf[:], w_uq_d.rearrange("(kc p) c -> p kc c", p=128))
                        K.dma("sp", wf2[:], w_ukv_d.rearrange("(kc p) c -> p kc c", p=128))
                        K.dma("sp", gq[:], qg_d)
                        K.dma("sp", gkv[:], kvg_d)
                        for fc in range(3):
                            K.ts("dve", wuq[:, fc, :], wf[:, fc, :], gq[:, fc:fc + 1], None, op0=ALU.mult)
                        for fc in range(2):
                            K.ts("dve", wukv[:, fc, :], wf2[:, fc, :], gkv[:, fc:fc + 1], None, op0=ALU.mult)
                        K.flush("p2w")

                    QT = K.sb("QT", [128, 8, S], BF16, stack=s2)
                    KT = K.sb("KT", [128, 8, LP], BF16, stack=s2)
                    Vaug = K.sb("Vaug", [128, NTT, 8, 65], BF16, stack=s2)
                    junk = K.sb("junk", [128, 512], F32, stack=s2)
                    ssq = K.sb("ssq", [128, 2], F32, stack=s2)
                    r2 = K.sb("r2", [128, 2], F32, stack=s2)
                    cqn = K.sb("cqn", [128, 640], F32, stack=s2)
                    cT = K.sb("cT", [128, 5, 128], BF16, stack=s2)
                    Qcat = K.sb("Qcat", [128, 8, 96], F32, stack=s2)
                    Kcat = K.sb("Kcat", [128, 8, 96], F32, stack=s2)
                    tmpr = K.sb("tmpr", [128, 8, 16], F32, stack=s2)
                    krr = K.sb("krr", [128, 32], F32, stack=s2)
                    sq = K.sb("sq", [128, 8, 96], F32, stack=s2)
                    n2 = K.sb("n2", [128, 16], F32, stack=s2)
                    qk2 = K.sb("qk2", [128, 16], F32, stack=s2)
                    K.memset("pool", qk2[:], 0.0)
                    K.memset("pool", Vaug[:, :, :, 64:65], 1.0)
                    for t in range(NTT):
                        issue_casts(2)
                        tok = slice(t * 128, (t + 1) * 128)
                        PA = bank(0, 0, 384)
                        PB = bank(0, 1, 288)
                        for kc in range(8):
                            K.mm(PA, hT[:, kc, tok], wm[:, kc, 0:384], start=(kc == 0), stop=(kc == 7))
                        for kc in range(8):
                            K.mm(PB, hT[:, kc, tok], wm[:, kc, 384:672], start=(kc == 0), stop=(kc == 7))
                        K.memset("pool", ssq[:], 0.0)
                        K.act(junk[:, 0:384], PA, AF.Square, accum=ssq[:, 0:1])
                        K.act(junk[:, 0:256], PB[:, 0:256], AF.Square, accum=ssq[:, 1:2])
                        K.ts("dve", r2[:, 0:1], ssq[:, 0:1], 1.0 / 384, EPS, op0=ALU.mult, op1=ALU.add)
                        K.ts("dve", r2[:, 1:2], ssq[:, 1:2], 1.0 / 256, EPS, op0=ALU.mult, op1=ALU.add)
                        K.act(r2[:], r2[:], AF.Sqrt)
                        K.recip(r2[:], r2[:])
                        K.ts("dve", cqn[:, 0:384], PA, r2[:, 0:1], None, op0=ALU.mult)
                        K.ts("dve", cqn[:, 384:640], PB[:, 0:256], r2[:, 1:2], None, op0=ALU.mult)
                        cos = cs_all[:, t, 0:16]
                        sin = cs_all[:, t, 16:32]
                        K.tt("dve", krr[:, 0:16], PB[:, 256:272], cos, ALU.mult)
                        K.tt("dve", tmpr[:, 0, :], PB[:, 272:288], sin, ALU.mult)
                        K.tt("dve", krr[:, 0:16], krr[:, 0:16], tmpr[:, 0, :], ALU.subtract)
                        K.tt("dve", krr[:, 16:32], PB[:, 256:272], sin, ALU.mult)
                        K.tt("dve", tmpr[:, 0, :], PB[:, 272:288], cos, ALU.mult)
                        K.tt("dve", krr[:, 16:32], krr[:, 16:32], tmpr[:, 0, :], ALU.add)
                        for j in range(5):
                            b = 0 if j < 4 else 1
                            K.tr(PS[1].s(b, (slice(None), slice(j * 128, (j + 1) * 128))), cqn[:, j * 128:(j + 1) * 128], ident[:])
                        K.copy("act", cT[:], PS[1][:, 0:640].re("p (a b) -> p a b", a=5))
                        if t >= 1:
                            for fc in range(3):
                                K.mm(bank(2, 0), cT[:, fc, :], wuq[:, fc, 0:512], start=(fc == 0), stop=(fc == 2))
                            for fc in range(3):
                                K.mm(bank(2, 1, 256), cT[:, fc, :], wuq[:, fc, 512:768], start=(fc == 0), stop=(fc == 2))
                        for fc in range(2):
                            K.mm(bank(3, 0), cT[:, 3 + fc, :], wukv[:, fc, 0:512], start=(fc == 0), stop=(fc == 1))
                        for fc in range(2):
                            K.mm(bank(3, 1), cT[:, 3 + fc, :], wukv[:, fc, 512:1024], start=(fc == 0), stop=(fc == 1))
                        kv3 = PS[3][:, 0:1024].re("p (h c) -> p h c", h=8)
                        K.copy("act", Kcat[:, :, 0:64], kv3[:, :, 0:64])
                        K.copy("pool", Kcat[:, :, 64:96], krr[:, :].bc(1, [128, 8, 32]))
                        K.copy("act", Vaug[:, t, :, 0:64], kv3[:, :, 64:128])
                        if t == 0:
                            K.ts("dve", Vaug[:, 0, :, :], Vaug[:, 0, :, :], padm[:, 0:1], None, op0=ALU.mult)
                        K.tt("pool", sq[:], Kcat[:], Kcat[:], ALU.mult)
                        K.reduce(n2[:, 8:16], sq[:], ALU.add)
                        K.tt("dve", qk2[:, 8:16], qk2[:, 8:16], n2[:, 8:16], ALU.max)
                        for h in range(8):
                            K.tr(PS[0].s(h // 4, (slice(0, 96), slice(h * 128, (h + 1) * 128))), Kcat[:, h, :], ident[:])
                        K.copy("act", KT[0:96, :, tok], PS[0][0:96, :].re("p (h c) -> p h c", h=8))
                        if t >= 1:
                            q3 = PS[2][:, 0:768].re("p (h c) -> p h c", h=8)
                            cosb = cos.bc(1, [128, 8, 16])
                            sinb = sin.bc(1, [128, 8, 16])
                            K.copy("act", Qcat[:, :, 0:64], q3[:, :, 0:64])
                            K.tt("dve", Qcat[:, :, 64:80], q3[:, :, 64:80], cosb, ALU.mult)
                            K.tt("dve", tmpr[:], q3[:, :, 80:96], sinb, ALU.mult)
                            K.tt("dve", Qcat[:, :, 64:80], Qcat[:, :, 64:80], tmpr[:], ALU.subtract)
                            K.tt("dve", Qcat[:, :, 80:96], q3[:, :, 64:80], sinb, ALU.mult)
                            K.tt("dve", tmpr[:], q3[:, :, 80:96], cosb, ALU.mult)
                            K.tt("dve", Qcat[:, :, 80:96], Qcat[:, :, 80:96], tmpr[:], ALU.add)
                            K.tt("pool", sq[:], Qcat[:], Qcat[:], ALU.mult)
                            K.reduce(n2[:, 0:8], sq[:], ALU.add)
                            K.tt("dve", qk2[:, 0:8], qk2[:, 0:8], n2[:, 0:8], ALU.max)
                            for h in range(8):
                                K.tr(PS[1].s(h // 4, (slice(0, 96), slice(h * 128, (h + 1) * 128))), Qcat[:, h, :], ident[:])
                            K.copy("act", QT[0:96, :, (t - 1) * 128:t * 128], PS[1][0:96, :].re("p (h c) -> p h c", h=8))
                    m16 = K.sb("m16", [16, 1], F32, stack=s2)
                    dg = K.sb("dg", [16, 16], F32, stack=s2)
                    bq = K.sb("bq", [128, 16], F32, stack=s2)
                    bb = K.sb("bb", [128, 8], F32, stack=s2)
                    K.tr(bank(0, 0)[0:16, 0:128], qk2[:, 0:16], ident[:])
                    K.reduce(m16[:], bank(0, 0)[0:16, 0:128], ALU.max)
                    K.ts("dve", dg[:], ident[0:16, 0:16], m16[:, 0:1], None, op0=ALU.mult)
                    K.mm(bank(0, 1)[:, 0:16], ones[0:16, :], dg[:])
                    K.copy("dve", bq[:], bank(0, 1)[:, 0:16])
                    K.tt("dve", bb[:], bq[:, 0:8], bq[:, 8:16], ALU.mult)
                    K.act(bb[:], bb[:], AF.Sqrt)
                    K.ts("dve", bb[:], bb[:], -ATT_SCALE, None, op0=ALU.mult)
                    NB = QG // 128
                    PT = [K.sb(f"PT{i}", [128, QG], BF16, stack=s2) for i in range(4)]
                    OT = [K.sb(f"OT{i}", [65, QG], F32, stack=s2) for i in range(2)]
                    Otok = K.sb("Otok", [128, NB, 512], F32, stack=s2)
                    rsum = K.sb("rsum", [128, NB], F32, stack=s2)
                    ss = K.sb("ss", [128, NB], F32, stack=s2)
                    sidx = 0
                    for qg in range(NQG):
                        qs = slice(qg * QG, (qg + 1) * QG)
                        items = [(h, kt) for h in range(8) for kt in range(NTT)]
                        LAG = 3
                        pend = []

                        def emit_pv(h, kt, pt):
                            PO = bank(3, h % 2, QG)[0:65, :]
                            K.mm(PO, Vaug[:, kt, h, :], pt[:], start=(kt == 0), stop=(kt == NTT - 1))
                            if kt == NTT - 1:
                                ot = OT[h % 2]
                                K.copy("dve", ot[:], PO)
                                PTr = bank(2, h % 2, NB * 65)
                                for b in range(NB):
                                    K.tr(PTr[:, b * 65:(b + 1) * 65], ot[0:65, b * 128:(b + 1) * 128], ident[0:65, 0:65])
                                p3 = PTr.re("p (b c) -> p b c", c=65)
                                K.recip(rsum[:], p3[:, :, 64])
                                K.tt("dve", Otok[:, :, h * 64:(h + 1) * 64], p3[:, :, 0:64], rsum[:].bc(2, [128, NB, 64]), ALU.mult)

                        for (h, kt) in items:
                            PSs = bank(sidx % 2, (sidx // 2) % 2, QG)
                            pt = PT[sidx % 4]
                            sidx += 1
                            K.mm(PSs, KT[0:96, h, kt * 128:(kt + 1) * 128], QT[0:96, h, qs])
                            K.act(pt[:], PSs, AF.Exp, scale=ATT_SCALE, bias=bb[:, h:h + 1])
                            pend.append((h, kt, pt))
                            if len(pend) > LAG:
                                emit_pv(*pend.pop(0))
                        while pend:
                            emit_pv(*pend.pop(0))
                        K.memset("pool", ss[:], 0.0)
                        for b in range(NB):
                            K.act(junk[:], Otok[:, b, :], AF.Square, accum=ss[:, b:b + 1])
                        K.ts("dve", ss[:], ss[:], 1.0 / 512, EPS, op0=ALU.mult, op1=ALU.add)
                        K.act(ss[:], ss[:], AF.Sqrt)
                        K.recip(ss[:], ss[:])
                        for b in range(NB):
                            K.ts("dve", Otok[:, b, :], Otok[:, b, :], ss[:, b:b + 1], None, op0=ALU.mult)
                            pb = bank(2, b % 2)
                            for fc in range(4):
                                K.tr(pb[:, fc * 128:(fc + 1) * 128], Otok[:, b, fc * 128:(fc + 1) * 128], ident[:])
                            t0 = qg * QG + b * 128
                            K.copy("act", yaT[:, :, t0:t0 + 128], pb.re("p (a b) -> p a b", a=4))
                    if seq == 0:
                        dump("yaT", yaT[:], [128, 4, S], BF16)
                        dump("QT", QT[0:96], [96, 8, S], BF16)
                        dump("KT", KT[0:96], [96, 8, LP], BF16)
                        dump("Vaug", Vaug[:], [128, NTT, 8, 65], BF16)
                    K.flush("p2_mla")

                s3 = ExitStack()
                BT = K.sb("BT", [128, 2, LP], BF16, stack=s3)
                CT = K.sb("CT", [128, 2, LP], BF16, stack=s3)
                Btok = K.sb("Btok", [128, NTT, 2, 128], BF16, stack=s3)
                DTa = K.sb("DTa", [128, NTT, 32], F32, stack=s3)
                YS = K.sb("YS", [128, NTT, 32], F32, stack=s3)
                WX = K.sb("WX", [128, NTT, 32], F32, stack=s3)
                CD = K.sb("CD", [128, NTT, 32], F32, stack=s3)
                AAb = K.sb("AAb", [128, NTT, 32], BF16, stack=s3)
                with ExitStack() as s3a:
                    wx = K.sb("wx", [128, 8, 1568], BF16, stack=s3a)
                    K.dma("pool", wx[:], w_in_d[:, SPLIT_Z:SPLIT_DT].rearrange("(kc p) c -> p kc c", p=128))
                    cw = K.sb("cw", [128, 12, 5], F32, stack=s3a)
                    cb = K.sb("cb", [128, 12], F32, stack=s3a)
                    K.dma("sp", cw[:], cw_d)
                    K.dma("sp", cb[:], cb_d)
                    xpres = [K.sb(f"xpre{i}", [128, LP + 4], F32, stack=s3a) for i in range(2)]
                    accs = [K.sb(f"acc{i}", [128, LP], F32, stack=s3a) for i in range(2)]
                    sils = [K.sb(f"sil{i}", [128, LP], F32, stack=s3a) for i in range(2)]
                    stg = [K.sb(f"stg{i}", [128, 4, 128], BF16, stack=s3a) for i in range(2)]
                    for xp_ in xpres:
                        K.memset("pool", xp_[:], 0.0)
                    groups = [(g0, min(g0 + 512, LP)) for g0 in range(0, LP, 512)]
                    pidx = 0
                    def conv_stage1(c):
                        nonlocal_p[0] = nonlocal_p[0]
                        issue_casts(2)
                        xpre = xpres[c % 2]
                        for (g0, g1) in groups:
                            pb = bank(nonlocal_p[0] % 4, (nonlocal_p[0] // 4) % 2, g1 - g0)
                            nonlocal_p[0] += 1
                            for kc in range(8):
                                K.mm(pb, wx[:, kc, c * 128:(c + 1) * 128], hT[:, kc, g0:g1], start=(kc == 0), stop=(kc == 7))
                            K.copy("act", xpre[:, 2 + g0:2 + g1], pb)
                        K.memset("pool", xpre[:, 2:114], 0.0)

                    def conv_stage2(c):
                        xpre = xpres[c % 2]
                        acc = accs[c % 2]
                        sil = sils[c % 2]
                        K.ts("dve", acc[:], xpre[:, 0:LP], cw[:, c, 0:1], None, op0=ALU.mult)
                        for k in range(1, 5):
                            K.stt(acc[:], xpre[:, k:k + LP], cw[:, c, k:k + 1], acc[:], ALU.mult, ALU.add)
                        K.act(sil[:], acc[:], AF.Silu, bias=cb[:, c:c + 1])

                    def conv_stage3(c):
                        sil = sils[c % 2]
                        if c >= 10:
                            K.copy("pool", CT[:, c - 10, :], sil[:])
                            return
                        if c >= 8:
                            K.copy("pool", BT[:, c - 8, :], sil[:])
                        for t0 in range(0, NTT, 4):
                            n = min(4, NTT - t0)
                            pb = bank(nonlocal_p[0] % 4, (nonlocal_p[0] // 4) % 2, n * 128)
                            nonlocal_p[0] += 1
                            for j in range(n):
                                K.tr(pb[:, j * 128:(j + 1) * 128], sil[:, (t0 + j) * 128:(t0 + j + 1) * 128], ident[:])
                            if c >= 8:
                                K.copy("act", Btok[:, t0:t0 + n, c - 8, :], pb.re("p (a b) -> p a b", b=128))
                            else:
                                sg = stg[(t0 // 4) % 2]
                                K.copy("act", sg[:, 0:n, :], pb.re("p (a b) -> p a b", b=128))
                                K.dma("sp", xs_s[t0:t0 + n, :, c * 128:(c + 1) * 128].rearrange("t p c -> p t c"), sg[:, 0:n, :],
                                      writes=[xs_r.ss(range(t0, t0 + n))], sem="xsw" + sg.name)

                    nonlocal_p = [pidx]
                    conv_stage1(0)
                    for c in range(12):
                        if c + 1 < 12:
                            conv_stage1(c + 1)
                        conv_stage2(c)
                        conv_stage3(c)
                    dtb = K.sb("dtb", [128, 32], F32, stack=s3a)
                    aneg = K.sb("aneg", [128, 32], F32, stack=s3a)
                    K.dma("sp", dtb[:], dtb_d.to_broadcast([128, 32]))
                    K.dma("sp", aneg[:], alog_d.to_broadcast([128, 32]))
                    K.act(aneg[:], aneg[:], AF.Exp)
                    K.ts("dve", aneg[:], aneg[:], -1.0, None, op0=ALU.mult)
                    AA = K.sb("AA", [128, NTT, 32], F32, stack=s3a)
                    X1 = K.sb("X1", [128, NTT, 32], F32, stack=s3a)
                    X2 = K.sb("X2", [128, NTT, 32], F32, stack=s3a)
                    TOT = K.sb("TOT", [128, NTT, 32], F32, stack=s3a)
                    W = NTT * 32
                    for t in range(NTT):
                        pb = PS[0].s(t // 16, (slice(None), slice(t * 32, (t + 1) * 32)))
                        for kc in range(8):
                            K.mm(pb, hT[:, kc, t * 128:(t + 1) * 128], wx[:, kc, 1536:1568], start=(kc == 0), stop=(kc == 7))
                    p3 = PS[0][:, 0:W].re("p (t c) -> p t c", c=32)
                    K.tt("dve", X1[:], p3, dtb[:].bc(1, [128, NTT, 32]), ALU.add)
                    K.act(X1[:], X1[:], AF.Exp)
                    K.act(DTa[:], X1[:], AF.Ln, bias=1.0)
                    K.ts("dve", DTa[:, 0, :], DTa[:, 0, :], padm[:, 0:1], None, op0=ALU.mult)
                    K.tt("dve", AA[:], DTa[:], aneg[:].bc(1, [128, NTT, 32]), ALU.mult)
                    K.copy("pool", AAb[:], AA[:])
                    for t in range(NTT):
                        K.mm(PS[1].s(t // 16, (slice(None), slice(t * 32, (t + 1) * 32))), mle[:], AA[:, t, :])
                        K.mm(PS[2].s(t // 16, (slice(None), slice(t * 32, (t + 1) * 32))), ones[:], AA[:, t, :])
                    K.copy("dve", X1[:], PS[1][:, 0:W].re("p (t c) -> p t c", c=32))
                    K.copy("act", TOT[:], PS[2][:, 0:W].re("p (t c) -> p t c", c=32))
                    K.tt("dve", X1[:, :, 16:32], X1[:, :, 16:32], AA[:, :, 16:32], ALU.subtract)
                    K.tt("dve", X2[:], TOT[:], X1[:], ALU.subtract)
                    K.act(X1[:], X1[:], AF.Exp)
                    K.act(X2[:], X2[:], AF.Exp)
                    K.act(CD[:], TOT[:], AF.Exp)
                    K.copy("pool", YS[:, :, 0:16], X1[:, :, 0:16])
                    K.copy("pool", YS[:, :, 16:32], X2[:, :, 16:32])
                    K.tt("dve", WX[:, :, 0:16], DTa[:, :, 0:16], X2[:, :, 0:16], ALU.mult)
                    K.tt("dve", WX[:, :, 16:32], DTa[:, :, 16:32], X1[:, :, 16:32], ALU.mult)
                    if seq == 0:
                        dump("BT", BT[:], [128, 2, LP], BF16)
                        dump("CT", CT[:], [128, 2, LP], BF16)
                        dump("Btok", Btok[:], [128, NTT, 2, 128], BF16)
                        dump("DTa", DTa[:], [128, NTT, 32])
                        dump("YS", YS[:], [128, NTT, 32])
                        dump("WX", WX[:], [128, NTT, 32])
                        dump("CD", CD[:], [128, NTT, 32])
                    K.flush("p3a_conv")
                with ExitStack() as s3c:
                    wz = K.sb("wz", [128, 8, 1024], BF16, stack=s3c)
                    K.dma("pool", wz[:], w_in_d[:, 0:SPLIT_Z].rearrange("(kc p) c -> p kc c", p=128))
                    zs = [K.sb(f"zs{i}", [128, 1024], BF16, stack=s3c) for i in range(2)]
                    for t in range(1, NTT):
                        issue_casts(2)
                        for half in range(2):
                            pb = bank(t % 2, half)
                            for kc in range(8):
                                K.mm(pb, hT[:, kc, t * 128:(t + 1) * 128], wz[:, kc, half * 512:(half + 1) * 512], start=(kc == 0), stop=(kc == 7))
                        z = zs[t % 2]
                        K.act(z[:], PS[t % 2][:, :], AF.Silu)
                        K.dma("sp", sz_s[t - 1], z[:], writes=[sz_r.s(t - 1)], sem="szw" + z.name)
                    K.flush("p3c_z")
                if seq == 0 and "xs" in dbg:
                    with ExitStack() as sd:
                        tmpx = K.sb("tmpx", [128, NTT, 1024], BF16, stack=sd)
                        K.dma("sp", tmpx[:], xs_s.rearrange("t p c -> p t c"), reads=[xs_r[:]] if False else [V(None, xs_r.regs)], sem="dbg")
                        dump("xs", tmpx[:], [128, NTT, 1024], BF16)
                        tmpz = K.sb("tmpz", [128, NT, 1024], BF16, stack=sd)
                        K.dma("sp", tmpz[:], sz_s.rearrange("t p c -> p t c"), reads=[V(None, sz_r.regs)], sem="dbg")
                        dump("sz", tmpz[:], [128, NT, 1024], BF16)
                        K.flush("dbgxs")

                with ExitStack() as s4:
                    xsb = [K.sb(f"xsb{i}", [128, 1024], BF16, stack=s4) for i in range(2)]
                    xw = [K.sb(f"xw{i}", [128, 1024], BF16, stack=s4) for i in range(2)]
                    Sb = K.sb("Sb", [128, 1024], F32, stack=s4)
                    tmpS = K.sb("tmpS", [128, 1024], F32, stack=s4)
                    sst = [K.sb(f"sst{i}", [128, 1024], BF16, stack=s4) for i in range(2)]
                    K.memset("pool", Sb[:], 0.0)
                    K.memset("pool", sst[0][:], 0.0)
                    K.dma("sp", sinb_s[NTT - 1], sst[0][:], writes=[sinb_r.s(NTT - 1)], sem="sbw0")
                    it = 0
                    for c in range(NTT - 1, 1, -1):
                        it += 1
                        xs_t = xsb[it % 2]
                        K.dma("sp", xs_t[:], xs_s[c], reads=[xs_r.s(c)], sem="xsl" + xs_t.name)
                        w = xw[it % 2]
                        K.tt("dve", w[:].re("p (h d) -> p h d", d=64), xs_t[:].re("p (h d) -> p h d", d=64),
                             WX[:, c, 16:32].bc(2, [128, 16, 64]), ALU.mult)
                        for g in range(2):
                            K.mm(bank(it % 2, g), Btok[:, c, g, :], w[:, g * 512:(g + 1) * 512])
                        K.tt("dve", tmpS[:].re("p (h d) -> p h d", d=64), Sb[:].re("p (h d) -> p h d", d=64),
                             CD[:, c, 16:32].bc(2, [128, 16, 64]), ALU.mult)
                        K.tt("dve", Sb[:], tmpS[:], PS[it % 2][:, :], ALU.add)
                        so = sst[it % 2]
                        K.copy("act", so[:], Sb[:])
                        K.dma("sp", sinb_s[c - 1], so[:], writes=[sinb_r.s(c - 1)], sem="sbw" + str(it % 2))
                    K.flush("p4_bwd")
                with ExitStack() as s5:
                    vecs4 = K.sb("vecs4", [128, 4, D], F32, stack=s5)
                    for i in range(4):
                        K.dma("sp", vecs4[:, i, :], vecs_d[i:i + 1, :].to_broadcast([128, D]))
                    vecs_box[0] = vecs4
                    xb[0] = K.sb("xb5", [128, D], F32, stack=s5)
                    hres = xb[0]
                    wo = K.sb("wo", [128, 12, D], BF16, stack=s5)
                    with ExitStack() as s5w:
                        wof = K.sb("wof", [128, 12, D], F32, stack=s5w)
                        ngs = K.sb("ngs", [128, 12], F32, stack=s5w)
                        K.dma("sp", wof[:], w_out_d.rearrange("(kc p) c -> p kc c", p=128))
                        K.dma("sp", ngs[:], ng_d)
                        for fc in range(12):
                            K.ts("dve" if fc % 2 else "pool", wo[:, fc, :], wof[:, fc, :], ngs[:, fc:fc + 1], None, op0=ALU.mult)
                        K.flush("p5w")
                    xsb = [K.sb(f"xsc{i}", [128, 1024], BF16, stack=s5) for i in range(2)]
                    szb = [K.sb(f"szb{i}", [128, 1024], BF16, stack=s5) for i in range(2)]
                    sbb = [K.sb(f"sbb{i}", [128, 1024], BF16, stack=s5) for i in range(2)]
                    xdt = [K.sb(f"xdt{i}", [128, 1024], BF16, stack=s5) for i in range(2)]
                    xwf = xdt[0]
                    Sf = K.sb("Sf", [128, 1024], F32, stack=s5)
                    Sfb = K.sb("Sfb", [128, 1024], BF16, stack=s5)
                    cbm = K.sb("cbm", [128, 2, 2, 128], BF16, stack=s5)
                    rhsD = [K.sb(f"rhsD{i}", [128, 16, 128], BF16, stack=s5) for i in range(2)]
                    Eb = [K.sb(f"Eb{i}", [128, 4, 128], BF16, stack=s5) for i in range(2)]
                    MT = K.sb("MT", [128, 2, 16, 128], BF16, stack=s5)
                    t1 = K.sb("t1", [128, 1024], F32, stack=s5)
                    t2 = K.sb("t2", [128, 1024], F32, stack=s5)
                    ssy = K.sb("ssy", [128, 1], F32, stack=s5)
                    ymT = K.sb("ymT", [128, 8, 128], BF16, stack=s5)
                    h1b = [K.sb("h1b0", [128, D], F32, stack=s5)] * 2
                    tmpS = t2
                    junk5 = t2
                    rr = t2
                    yb = t1
                    K.memset("pool", Sf[:], 0.0)
                    K.memset("pool", Sfb[:], 0.0)
                    r3 = lambda v: v.re("p (h d) -> p h d", d=64)
                    def p5_loads(c):
                        xs_t = xsb[c % 2]
                        K.dma("sp", xs_t[:], xs_s[c], reads=[xs_r.s(c)], sem="xsl" + xs_t.name)
                        if c >= 1:
                            sz_t = szb[c % 2]
                            K.dma("sp", sz_t[:], sz_s[c - 1], reads=[sz_r.s(c - 1)], sem="szl" + sz_t.name)
                            sb_t = sbb[c % 2]
                            K.dma("sp", sb_t[:], sinb_s[c], reads=[sinb_r.s(c)], sem="sbl" + sb_t.name)

                    def p5_A(c):
                        tok = slice(c * 128, (c + 1) * 128)
                        xs_t = xsb[c % 2]
                        for g in range(2):
                            K.mm(bank(1, 0)[:, g * 128:(g + 1) * 128], BT[:, g, tok], CT[:, g, tok])
                        cb3 = bank(1, 0)[:, 0:256].re("p (g l) -> p g l", g=2)
                        K.tt("dve", cbm[:, 0, :, :], cb3, mle[:].bc(1, [128, 2, 128]), ALU.mult)
                        K.tt("dve", cbm[:, 1, :, :], cb3, mge[:].bc(1, [128, 2, 128]), ALU.mult)
                        for d in range(2):
                            hs = slice(d * 16, (d + 1) * 16)
                            msk = mle if d == 0 else mge
                            lt = mgt if d == 0 else mlt
                            K.tt("dve", rhsD[d][:], AAb[:, c, hs].bc(2, [128, 16, 128]), msk[:].bc(1, [128, 16, 128]), ALU.mult)
                            K.tt("dve", r3(xdt[d][:]), r3(xs_t[:]), DTa[:, c, hs].bc(2, [128, 16, 64]), ALU.mult)
                            for q4 in range(4):
                                pb = bank(0, didx[0] % 2)
                                eb = Eb[didx[0] % 2]
                                didx[0] += 1
                                K.mm(pb, lt[:], rhsD[d][:, q4 * 4:(q4 + 1) * 4, :].re("p a b -> p (a b)"))
                                K.act(eb[:].re("p a b -> p (a b)"), pb, AF.Exp)
                                K.tt("dve", MT[:, d, q4 * 4:(q4 + 1) * 4, :], eb[:], cbm[:, d, q4 // 2, :].bc(1, [128, 4, 128]), ALU.mult)
                        for h in range(16):
                            yo_ = PS[2].s(h // 8, (slice(None), slice(h * 64, (h + 1) * 64)))
                            K.mm(yo_, MT[:, 0, h, :], xdt[0][:, h * 64:(h + 1) * 64], start=True, stop=False)
                            K.mm(yo_, MT[:, 1, h, :], xdt[1][:, h * 64:(h + 1) * 64], start=False, stop=True)

                    def p5_B1(c):
                        tok = slice(c * 128, (c + 1) * 128)
                        xs_t = xsb[c % 2]
                        sz_t = szb[c % 2]
                        sb_t = sbb[c % 2]
                        for g in range(2):
                            K.mm(bank(3, g), CT[:, g, tok], Sfb[:, g * 512:(g + 1) * 512])
                        K.tt("dve", r3(t1[:]), r3(PS[3][:, :]), YS[:, c, 0:16].bc(2, [128, 16, 64]), ALU.mult)
                        for g in range(2):
                            K.mm(bank(3, g), CT[:, g, tok], sb_t[:, g * 512:(g + 1) * 512])
                        K.tt("dve", r3(t2[:]), r3(PS[3][:, :]), YS[:, c, 16:32].bc(2, [128, 16, 64]), ALU.mult)
                        K.tt("pool", t1[:], t1[:], t2[:], ALU.add)
                        K.tt("dve", r3(t2[:]), r3(xs_t[:]), dsk[:].bc(2, [128, 16, 64]), ALU.mult)
                        K.tt("pool", t1[:], t1[:], t2[:], ALU.add)
                        K.tt("dve", yb[:], PS[2][:, :], t1[:], ALU.add)
                        K.tt("dve", yb[:], yb[:], sz_t[:], ALU.mult)
                        K.memset("pool", ssy[:], 0.0)
                        K.act(junk5[:], yb[:], AF.Square, accum=ssy[:, 0:1])
                        K.ts("dve", ssy[:], ssy[:], 1.0 / 1024, EPS, op0=ALU.mult, op1=ALU.add)
                        K.act(ssy[:], ssy[:], AF.Sqrt)
                        K.recip(ssy[:], ssy[:])
                        K.ts("dve", yb[:], yb[:], ssy[:, 0:1], None, op0=ALU.mult)
                        if seq == 0 and c == 1:
                            dump("yb", yb[:], [128, 1024])

                    def p5_B2(c):
                        for half in range(2):
                            pb = bank(1, 1) if half == 0 else bank(1, 0)
                            for j in range(4):
                                K.tr(pb[:, j * 128:(j + 1) * 128], yb[:, (half * 4 + j) * 128:(half * 4 + j + 1) * 128], ident[:])
                            K.copy("act", ymT[:, half * 4:half * 4 + 4, :], pb.re("p (a b) -> p a b", a=4))
                        for half in range(2):
                            pb = bank(3, half)
                            for fc in range(12):
                                lhs = ymT[:, fc, :] if fc < 8 else yaT[:, fc - 8, (c - 1) * 128:c * 128]
                                K.mm(pb, lhs, wo[:, fc, half * 512:(half + 1) * 512], start=(fc == 0), stop=(fc == 11))
                        load_x(seq, c, xb[0])
                        ln_tile(xb[0][:], hres[:], 0, 1)
                        K.stt(rr[:], hres[:], ALPHA, PS[3][:, :], ALU.mult, ALU.add)
                        if seq == 0 and c == 1:
                            dump("rr", rr[:], [128, 1024])
                        ho = h1b[c % 2]
                        ln_tile(rr[:], ho[:], 2, 3)
                        K.dma("sp", h1_s[seq, (c - 1) * 128:c * 128, :], ho[:], writes=[h1_r.s(seq * NT + c - 1)], sem="h1w" + str(c % 2))

                    def p5_S(c):
                        xs_t = xsb[c % 2]
                        K.tt("dve", r3(xwf[:]), r3(xs_t[:]), WX[:, c, 0:16].bc(2, [128, 16, 64]), ALU.mult)
                        for g in range(2):
                            K.mm(bank(3, g), Btok[:, c, g, :], xwf[:, g * 512:(g + 1) * 512])
                        K.tt("dve", r3(tmpS[:]), r3(Sf[:]), CD[:, c, 0:16].bc(2, [128, 16, 64]), ALU.mult)
                        K.tt("dve", Sf[:], tmpS[:], PS[3][:, :], ALU.add)
                        K.copy("act", Sfb[:], Sf[:])

                    didx = [0]
                    p5_loads(0)
                    p5_loads(1)
                    p5_A(1)
                    for c in range(NTT):
                        if c >= 1:
                            p5_B1(c)
                        if c < NTT - 1:
                            p5_S(c)
                        if c + 2 < NTT:
                            p5_loads(c + 2)
                        if c >= 1 and c + 1 < NTT:
                            p5_A(c + 1)
                        if c >= 1:
                            p5_B2(c)
                    K.flush("p5_ssd")
                if seq == 0:
                    issue_casts(64)
                    K.flush("castfin")
                    cst_box[0] = None
                s3.close()

        NEG = -1.0e30
        with ExitStack() as s6:
            iota = K.sb("iota", [128, 128], F32, stack=s6)
            K.dma("sp", iota[:], iota_d)
            iotab = K.sb("iotab", [128, 128], BF16, stack=s6)
            K.copy("dve", iotab[:], iota[:])
            wq = K.sb("wq", [128, 8, 2048], BF16, stack=s6)
            K.dma("pool", wq[:], wq_d.rearrange("(kc p) c -> p kc c", p=128))
            skT = K.sb("skT", [128, 16, 128], F32, stack=s6)
            K.dma("sp", skT[:], skT_d)
            vecs6 = K.sb("vecs6", [128, 2, D], F32, stack=s6)
            for i in range(2):
                K.dma("sp", vecs6[:, i, :], vecs_d[4 + i:5 + i, :].to_broadcast([128, D]))
            vecs_box[0] = vecs6
            h1s = [K.sb(f"h1_{i}", [128, 2, D], F32, stack=s6) for i in range(2)]
            h1Ts = [K.sb(f"h1T_{i}", [128, 8, 256], BF16, stack=s6) for i in range(2)]
            ITss = [K.sb(f"ITs_{i}", [128, 2, 3, 128], F32, stack=s6) for i in range(2)]
            GTb = K.sb("GTb", [128, 256, 128], BF16, stack=s6)
            tiles = [(sq, t) for sq in range(NSEQ) for t in range(NT)]
            groups = [tiles[gi:gi + 2] for gi in range(0, len(tiles), 2)]
            NGRP = len(groups)

            def alloc_route(stack):
                R = {}
                R["qT"] = K.sb("qT", [128, 16, 128], F32, stack=stack)
                R["sc"] = [K.sb("sc0", [128, 16, 128], F32, nreg=16, stack=stack)] * 2
                R["V1"] = K.sb("V1", [128, 16, 16], F32, nreg=32, stack=stack)
                R["I1"] = K.sb("I1", [128, 16, 16], U32, nreg=32, stack=stack)
                R["I1f"] = K.sb("I1f", [128, 16, 16], F32, stack=stack)
                R["cand"] = K.sb("cand", [128, 8, 256], F32, nreg=8, stack=stack)
                R["TS"] = K.sb("TS", [128, 8, 16], F32, nreg=16, stack=stack)
                R["SEL"] = K.sb("SEL", [128, 8, 16], U32, nreg=16, stack=stack)
                R["J1"] = K.sb("J1", [128, 8, 16], U32, stack=stack)
                R["J2"] = K.sb("J2", [128, 8, 16], U32, stack=stack)
                R["J1f"] = K.sb("J1f", [128, 8, 16], F32, stack=stack)
                R["J2f"] = K.sb("J2f", [128, 8, 16], F32, stack=stack)
                R["EG"] = K.sb("EG", [128, 3, 128], F32, stack=stack)
                R["gs"] = K.sb("gs", [128, 8], F32, stack=stack)
                return R

            def topk16_gen(n, src, vals, idxs):
                for f in range(n):
                    K.op("dve", lambda e, f=f: e.max(out=vals.h[:, f, 0:8], in_=src.h[:, f, :]), reads=[src.s(f)], writes=[vals.s(2 * f)])
                    if f % 4 == 3:
                        yield
                for f in range(n):
                    K.op("dve", lambda e, f=f: e.max_index(out=idxs.h[:, f, 0:8], in_max=vals.h[:, f, 0:8], in_values=src.h[:, f, :]),
                         reads=[src.s(f), vals.s(2 * f)], writes=[idxs.s(2 * f)])
                    if f % 4 == 3:
                        yield
                for f in range(n):
                    K.op("dve", lambda e, f=f: e.match_replace(out=src.h[:, f, :], in_to_replace=vals.h[:, f, 0:8], in_values=src.h[:, f, :], imm_value=NEG),
                         reads=[vals.s(2 * f)], writes=[src.s(f)])
                    if f % 4 == 3:
                        yield
                for f in range(n):
                    K.op("dve", lambda e, f=f: e.max(out=vals.h[:, f, 8:16], in_=src.h[:, f, :]), reads=[src.s(f)], writes=[vals.s(2 * f + 1)])
                    if f % 4 == 3:
                        yield
                for f in range(n):
                    K.op("dve", lambda e, f=f: e.max_index(out=idxs.h[:, f, 8:16], in_max=vals.h[:, f, 8:16], in_values=src.h[:, f, :]),
                         reads=[src.s(f), vals.s(2 * f + 1)], writes=[idxs.s(2 * f + 1)])
                    if f % 4 == 3:
                        yield

            IDLE = 24

            def routing_gen(g, R):
                par = g % 2
                grp = groups[g]
                h1 = h1s[par]
                h1T = h1Ts[par]
                ITs = ITss[par]
                qT, scs, V1, I1, I1f, cand, TS, SEL = R["qT"], R["sc"], R["V1"], R["I1"], R["I1f"], R["cand"], R["TS"], R["SEL"]
                J1, J2, J1f, J2f, EG, gs = R["J1"], R["J2"], R["J1f"], R["J2f"], R["EG"], R["gs"]
                qT3 = qT[:]
                oh4 = cand[:].re("p h (k j) -> p h k j", k=16)
                for j, (sq, t) in enumerate(grp):
                    K.dma("sp", h1[:, j, :], h1_s[sq, t * 128:(t + 1) * 128, :], reads=[h1_r.s(sq * NT + t)], sem="h1l" + str(par))
                    for half in range(2):
                        pb = bank(1, half)
                        for q in range(4):
                            kc = half * 4 + q
                            K.tr(pb[:, q * 128:(q + 1) * 128], h1[:, j, kc * 128:(kc + 1) * 128], ident[:])
                        K.copy("act", h1T[:, half * 4:half * 4 + 4, j * 128:(j + 1) * 128], pb.re("p (a b) -> p a b", a=4))
                        yield
                for j in range(len(grp)):
                    tk = slice(j * 128, (j + 1) * 128)
                    sc = scs[j % 2]
                    for f in range(16):
                        pb = bank(1, f % 2, 128)
                        for kc in range(8):
                            K.mm(pb, wq[:, kc, f * 128:(f + 1) * 128], h1T[:, kc, tk], start=(kc == 0), stop=(kc == 7))
                        K.copy("act", qT3[:, f, :], pb)
                        yield
                    if j >= 1:
                        for _ in range(IDLE):
                            yield
                    for hh in range(2):
                        for f8 in range(8):
                            f = hh * 8 + f8
                            K.mm(PS[1].s(f8 // 4, (slice(None), slice(f8 * 128, (f8 + 1) * 128))), qT3[:, f, :], skT[:, f, :])
                        K.copy("act", sc.ss(range(hh * 8, hh * 8 + 8), (slice(None), slice(hh * 8, hh * 8 + 8), slice(None))),
                               PS[1][:, :].re("p (a b) -> p a b", a=8))
                        yield
                    yield from topk16_gen(16, sc, V1, I1)
                    K.copy("dve", I1f[:], I1[:])
                    V4 = V1[:].re("p (h a) j -> p h a j", a=2)
                    I4 = I1f[:].re("p (h a) j -> p h a j", a=2)
                    c4 = cand[:].re("p h (a b) -> p h a b", a=16)
                    K.tt("dve", c4, V4[:, :, 0, :].bc(3, [128, 8, 16, 16]), V4[:, :, 1, :].bc(2, [128, 8, 16, 16]), ALU.add)
                    yield
                    yield from topk16_gen(8, cand, TS, SEL)
                    G3 = EG[:, 2, :].re("p (h k) -> p h k", h=8)
                    K.tt("dve", G3, TS[:], TS[:, :, 0].bc(2, [128, 8, 16]), ALU.subtract)
                    for _ in range(IDLE):
                        yield
                    K.act(G3, G3, AF.Exp)
                    K.reduce(gs[:], G3, ALU.add)
                    K.recip(gs[:], gs[:])
                    K.tt("dve", G3, G3, gs[:].bc(2, [128, 8, 16]), ALU.mult)
                    yield
                    K.op("dve", lambda e: e.tensor_single_scalar(out=J1.h[:], in_=SEL.h[:], scalar=4, op=ALU.logical_shift_right), reads=[SEL[:]], writes=[J1[:]])
                    K.op("dve", lambda e: e.tensor_single_scalar(out=J2.h[:], in_=SEL.h[:], scalar=15, op=ALU.bitwise_and), reads=[SEL[:]], writes=[J2[:]])
                    K.copy("dve", J1f[:], J1[:])
                    K.copy("dve", J2f[:], J2[:])
                    yield
                    for which, (Jf, half) in enumerate(((J1f, 0), (J2f, 1))):
                        K.tt("dve", oh4, Jf[:].bc(3, [128, 8, 16, 16]),
                             V(iota.h[:, 0:16].unsqueeze(1).unsqueeze(1).to_broadcast([128, 8, 16, 16]), iota.regs), ALU.is_equal)
                        K.tt("dve", oh4, oh4, I4[:, :, half, :].bc(2, [128, 8, 16, 16]), ALU.mult)
                        K.reduce(EG[:, which, :].re("p (h k) -> p h k", h=8), oh4, ALU.add)
                        yield
                    for _ in range(IDLE):
                        yield
                    for w3 in range(3):
                        K.tr(bank(1, 0)[:, w3 * 128:(w3 + 1) * 128], EG[:, w3, :], ident[:])
                    K.copy("act", ITs[:, j, :, :], bank(1, 0)[:, 0:384].re("p (a b) -> p a b", a=3))
                    if g == 0 and j == 0:
                        dump("EG", EG[:], [128, 3, 128])
                    yield

            with ExitStack() as sr0:
                R0 = alloc_route(sr0)
                for _ in routing_gen(0, R0):
                    pass
                K.flush("r_route")

            for g in range(NGRP):
                grp = groups[g]
                par = g % 2
                h1 = h1s[par]
                h1T = h1Ts[par]
                ITs = ITss[par]
                with ExitStack() as sg:
                    A1s = [K.sb(f"A1_{i}", [128, 32, 128], BF16, stack=sg) for i in range(2)]
                    A2s = [K.sb(f"A2_{i}", [128, 32, 128], BF16, nreg=32, stack=sg) for i in range(2)]
                    pidx = 0
                    hidx = 0
                    for j in range(len(grp)):
                        for hf in range(4):
                            A1 = A1s[hidx % 2]
                            A2 = A2s[hidx % 2]
                            hidx += 1
                            ts_ = slice(hf * 32, (hf + 1) * 32)
                            io3 = V(iota.h[:].unsqueeze(1).to_broadcast([128, 32, 128]), iota.regs)
                            K.tt("dve", A1[:], io3, ITs[:, j, 0, ts_].bc(2, [128, 32, 128]), ALU.is_equal)
                            for tq in range(32):
                                tcol = hf * 32 + tq
                                K.ts("dve", A2.s(tq, (slice(None), tq, slice(None))), iotab[:], ITs[:, j, 1, tcol:tcol + 1], ITs[:, j, 2, tcol:tcol + 1],
                                     op0=ALU.is_equal, op1=ALU.mult)
                            for t4 in range(8):
                                pb = bank(pidx % 4, (pidx // 4) % 2)
                                pidx += 1
                                for q in range(4):
                                    tt_ = t4 * 4 + q
                                    K.mm(pb[:, q * 128:(q + 1) * 128], A2.s(tt_, (slice(None), tt_, slice(None))), A1[:, tt_, :])
                                t0 = j * 128 + hf * 32 + t4 * 4
                                K.copy("act", GTb[:, t0:t0 + 4, :], pb.re("p (t i) -> p t i", t=4))
                    K.flush("g_build")
                with ExitStack() as sd:
                    ub = [K.sb(f"ub{i}", [128, 2, 1024], BF16, stack=sd) for i in range(2)]
                    vb = [K.sb(f"vb{i}", [128, 2, 1024], BF16, stack=sd) for i in range(2)]
                    ga = [K.sb(f"ga{i}", [128, 256], BF16, stack=sd) for i in range(4)]
                    cT = [K.sb(f"cT{i}", [128, 256], BF16, stack=sd) for i in range(4)]
                    r6 = K.sb("r6", [128, D], F32, stack=sd)
                    gen = None
                    if g + 1 < NGRP:
                        Rn = alloc_route(sd)
                        gen = routing_gen(g + 1, Rn)
                    NTK = len(grp) * 128
                    LAGD = 2
                    pend = []

                    def emit_v(ec, c_, v_t, jj):
                        for j in range(len(grp)):
                            for half in range(2):
                                K.mm(bank(2 + j, half), c_[:, j * 128:(j + 1) * 128], v_t[:, jj, half * 512:(half + 1) * 512],
                                     start=(ec == 0), stop=(ec == 127))

                    for eb in range(64):
                        u_t = ub[eb % 2]
                        v_t = vb[eb % 2]
                        K.dma("sp", u_t[:], uT_s[eb * 256:(eb + 1) * 256, :].rearrange("(j p) c -> p j c", p=128), reads=[uv_r.s(0)], sem="ul" + str(eb % 2))
                        K.dma("sp", v_t[:], v_s[eb * 256:(eb + 1) * 256, :].rearrange("(j p) c -> p j c", p=128), reads=[uv_r.s(1)], sem="vl" + str(eb % 2))
                        for jj in range(2):
                            ec = eb * 2 + jj
                            pa = bank(0, ec % 2, NTK)
                            for kc in range(8):
                                K.mm(pa, u_t[:, jj, kc * 128:(kc + 1) * 128], h1T[:, kc, 0:NTK], start=(kc == 0), stop=(kc == 7))
                            g_ = ga[ec % 4]
                            c_ = cT[ec % 4]
                            K.act(g_[:, 0:NTK], pa, AF.Gelu)
                            K.tt("pool", c_[:, 0:NTK], g_[:, 0:NTK], GTb[:, 0:NTK, ec], ALU.mult)
                            pend.append((ec, c_, v_t, jj))
                            if len(pend) > LAGD:
                                emit_v(*pend.pop(0))
                            if gen is not None:
                                for _ in range(3):
                                    if next(gen, "done") == "done":
                                        gen = None
                                        break
                    while pend:
                        emit_v(*pend.pop(0))
                    if gen is not None:
                        for _ in gen:
                            pass
                    for j, (sq, t) in enumerate(grp):
                        K.stt(r6[:], h1[:, j, :], ALPHA, PS[2 + j][:, :], ALU.mult, ALU.add)
                        if g == 0 and j == 0:
                            dump("r6", r6[:], [128, D])
                        ln_tile(r6[:], r6[:], 0, 1)
                        K.dma("sp", y_d[sq, t * 128:(t + 1) * 128, :], r6[:], sem="yw")
                    K.flush("d_dense")
    return nc, dbg_out


_CACHE = {}


def _host_consts(S):
    LP = S + 128
    pos = np.arange(LP, dtype=np.float32) - 112.0
    inv = (10000.0 ** (-np.arange(16, dtype=np.float32) / 16)).astype(np.float32)
    ang = pos[:, None] * inv[None, :]
    cossin = np.concatenate([np.cos(ang), np.sin(ang)], 1).astype(np.float32)
    ar = np.arange(128)
    return dict(
        cossin=cossin,
        ident=np.eye(128, dtype=np.float32),
        padmask=(ar >= 112).astype(np.float32).reshape(128, 1),
        mask_le=(ar[:, None] <= ar[None, :]).astype(np.float32),
        mask_ge=(ar[:, None] >= ar[None, :]).astype(np.float32),
        iota=np.ascontiguousarray(np.broadcast_to(np.arange(128, dtype=np.float32), (128, 128))),
    )


def _prep_weights(inp):
    f = lambda a: np.ascontiguousarray(np.asarray(a, dtype=np.float32))
    vecs = np.zeros((8, 1024), np.float32)
    for i, k in enumerate(["ln_in_g", "ln_in_b", "ln1_g", "ln1_b", "ln2_g", "ln2_b"]):
        vecs[i] = np.asarray(inp[k]).reshape(-1)
    return dict(
        meta=f(inp["meta_tokens"]), vecs=vecs, w_in=f(inp["w_in"][0]), w_uq=f(inp["w_uq"][0]), w_ukv=f(inp["w_ukv"][0]),
        q_norm_g=f(np.asarray(inp["q_norm_g"][0]).reshape(3, 128).T), kv_norm_g=f(np.asarray(inp["kv_norm_g"][0]).reshape(2, 128).T),
        conv_w=f(np.asarray(inp["conv_w"][0]).T.reshape(12, 128, 5).transpose(1, 0, 2)),
        conv_b=f(np.asarray(inp["conv_b"][0]).reshape(12, 128).T),
        dt_bias=f(np.concatenate([inp["dt_bias_fwd"][0], inp["dt_bias_bwd"][0]]).reshape(1, 32)),
        a_log=f(np.concatenate([inp["a_log_fwd"][0], inp["a_log_bwd"][0]]).reshape(1, 32)),
        d_skip=f(np.asarray(inp["d_skip"][0]).reshape(1, 16)),
        w_out=f(inp["w_out"][0]),
        norm_g=f(np.concatenate([inp["ssm_norm_g"][0], inp["attn_norm_g"][0]]).reshape(12, 128).T),
        peer_wq=f(inp["peer_w_query"][0]),
        peer_skT=f(np.asarray(inp["peer_sub_keys"][0]).transpose(3, 1, 0, 2).reshape(128, 16, 128)),
        peer_uT=f(np.asarray(inp["peer_u"][0]).reshape(128, 128, 8, 128).transpose(0, 3, 2, 1).reshape(128 * 128, 1024)),
        peer_v=f(inp["peer_v"][0]),
    )


def kernel(**inp):
    xp = np.asarray(inp["x_prompt"], dtype=np.float32)
    xs = np.asarray(inp["x_sample"], dtype=np.float32)
    S = xp.shape[1]
    allx = np.concatenate([xp, xs], 0)
    ncores = 8
    nseq = allx.shape[0] // ncores
    key = (S, nseq)
    if key not in _CACHE:
        _CACHE[key] = build(S, nseq)[0]
    nc = _CACHE[key]
    base = _prep_weights(inp)
    base.update(_host_consts(S))
    in_maps = []
    for c in range(ncores):
        m = dict(base)
        m["x"] = np.ascontiguousarray(allx[c * nseq:(c + 1) * nseq])
        in_maps.append(m)
    res = run_bass_kernel_spmd(nc, in_maps, core_ids=list(range(ncores)))
    ys = np.concatenate([np.asarray(r["y"]) for r in res.results], 0)
    nb = xp.shape[0]
    return (np.ascontiguousarray(ys[:nb]), np.ascontiguousarray(ys[nb:]))
```

```python
import numpy as np
import concourse.bass as bass
import concourse.mybir as mybir
from concourse.bass_utils import run_bass_kernel_spmd
from contextlib import ExitStack

F32 = mybir.dt.float32
BF16 = mybir.dt.bfloat16
U32 = mybir.dt.uint32
I32 = mybir.dt.int32
AF = mybir.ActivationFunctionType
ALU = mybir.AluOpType
AX = mybir.AxisListType

SAME_ENGINE_SYNC = True


class Reg:
    __slots__ = ("name", "w", "r")

    def __init__(self, name):
        self.name = name
        self.w = []
        self.r = []


class V:
    __slots__ = ("ap", "regs")

    def __init__(self, ap, regs):
        self.ap = ap
        self.regs = regs

    def __getitem__(self, key):
        return V(self.ap[key], self.regs)

    def re(self, s, **kw):
        return V(self.ap.rearrange(s, **kw), self.regs)

    def bc(self, axis, shape):
        return V(self.ap.unsqueeze(axis).to_broadcast(list(shape)), self.regs)

    def bcast(self, shape):
        return V(self.ap.to_broadcast(list(shape)), self.regs)

    def bitcast(self, dt):
        return V(self.ap.bitcast(dt), self.regs)


class T:
    def __init__(self, handle, name, nreg=1):
        self.h = handle
        self.name = name
        self.regs = [Reg(f"{name}.{i}") for i in range(nreg)]

    def __getitem__(self, key):
        return V(self.h[key], self.regs)

    def s(self, i, key=None):
        ap = None if self.h is None else (self.h[key] if key is not None else self.h[:])
        return V(ap, [self.regs[i]])

    def ss(self, idxs, key=None):
        ap = None if self.h is None else (self.h[key] if key is not None else self.h[:])
        return V(ap, [self.regs[i] for i in idxs])


ENG = ["pe", "act", "dve", "pool", "sp"]
BLOCKNAME = dict(pe="tensor", act="scalar", dve="vector", pool="gpsimd", sp="sync")


class KB:
    def __init__(self, nc):
        self.nc = nc
        self.engs = dict(pe=nc.tensor, act=nc.scalar, dve=nc.vector, pool=nc.gpsimd, sp=nc.sync)
        self.sem = {k: nc.alloc_semaphore("s_" + k) for k in ENG}
        self.cnt = {k: 0 for k in ENG}
        self.known = {k: {} for k in ENG}
        self.stream = {k: [] for k in ENG}
        self.dcnt = {}
        self.stack = ExitStack()
        self.nops = 0

    def sb(self, name, shape, dtype, nreg=1, stack=None):
        self.uid = getattr(self, "uid", 0) + 1
        h = (stack or self.stack).enter_context(self.nc.sbuf_tensor(f"sb{self.uid}_{name}", list(shape), dtype))
        return T(h, name, nreg)

    def psum(self, name, shape, dtype=F32, nreg=1, stack=None):
        self.uid = getattr(self, "uid", 0) + 1
        h = (stack or self.stack).enter_context(self.nc.psum_tensor(f"pp{self.uid}_{name}", list(shape), dtype))
        return T(h, name, nreg)

    def dsem(self, name):
        if name not in self.dcnt:
            self.sem[name] = self.nc.alloc_semaphore("d_" + name)
            self.dcnt[name] = 0
        return name

    def _deps(self, eng, reads, writes):
        deps = []
        for v in reads:
            for r in v.regs:
                deps += r.w
        for v in writes:
            for r in v.regs:
                deps += r.w
                deps += r.r
        need = {}
        for (sk, val) in deps:
            if sk == eng and (eng == "pe" or eng == "sp" or not SAME_ENGINE_SYNC):
                continue
            if self.known[eng].get(sk, 0) >= val:
                continue
            if need.get(sk, 0) < val:
                need[sk] = val
        for sk, val in need.items():
            self.stream[eng].append(("w", sk, val))
            self.known[eng][sk] = val

    def _commit(self, ev, reads, writes):
        for v in writes:
            for r in v.regs:
                r.w = [ev]
                r.r = []
        for v in reads:
            for r in v.regs:
                r.r = [e for e in r.r if e[0] != ev[0]]
                r.r.append(ev)

    def op(self, eng, fn, reads=(), writes=()):
        self._deps(eng, reads, writes)
        self.cnt[eng] += 1
        ev = (eng, self.cnt[eng])
        self.stream[eng].append(("o", fn))
        self._commit(ev, reads, writes)
        self.nops += 1

    def dma(self, q, out, in_, reads=(), writes=(), sem=None, **kw):
        reads = list(reads)
        writes = list(writes)
        if isinstance(out, V):
            writes.append(out)
            out = out.ap
        if isinstance(in_, V):
            reads.append(in_)
            in_ = in_.ap
        if sem is None:
            sem = "L_" + writes[0].regs[0].name.split(".")[0]
        self.dsem(sem)
        self._deps(q, reads, writes)
        if "throttle" in kw:
            lag = kw.pop("throttle")
            tv = self.dcnt[sem] - 16 * lag
            if tv > 0 and self.known[q].get(sem, 0) < tv:
                self.stream[q].append(("w", sem, tv))
                self.known[q][sem] = tv
        self.dcnt[sem] += 16
        ev = (sem, self.dcnt[sem])
        self.stream[q].append(("d", out, in_, sem, kw))
        self._commit(ev, reads, writes)
        self.nops += 1

    def flush(self, label=None):
        self.marks = getattr(self, "marks", [])
        self.marks.append((label, dict(self.cnt)))
        for e in ENG:
            for k in ENG:
                if k != e and k != "sp" and self.known[e].get(k, 0) < self.cnt[k]:
                    self.stream[e].append(("w", k, self.cnt[k]))
                    self.known[e][k] = self.cnt[k]
            for k, c in self.dcnt.items():
                if self.known[e].get(k, 0) < c:
                    self.stream[e].append(("w", k, c))
                    self.known[e][k] = c
        nc = self.nc
        with nc.Block() as block:
            for e in ENG:
                items = self.stream[e]
                engine = self.engs[e]
                semh = self.sem[e]

                def body(_e, items=items, engine=engine, semh=semh):
                    for it in items:
                        if it[0] == "w":
                            engine.wait_ge(self.sem[it[1]], it[2])
                        elif it[0] == "o":
                            it[1](engine).then_inc(semh, 1)
                        else:
                            _, out, in_, sem, kw = it
                            engine.dma_start(out=out, in_=in_, **kw).then_inc(self.sem[sem], 16)

                getattr(block, BLOCKNAME[e])(body)
        self.stream = {k: [] for k in ENG}

    def mm(self, out, lhsT, rhs, start=True, stop=True):
        self.op("pe", lambda e: e.matmul(out.ap, lhsT.ap, rhs.ap, start=start, stop=stop),
                reads=[lhsT, rhs], writes=[out])

    def tr(self, out, in_, ident):
        self.op("pe", lambda e: e.transpose(out.ap, in_.ap, ident.ap), reads=[in_, ident], writes=[out])

    def act(self, out, in_, func, bias=None, scale=None, accum=None, eng="act"):
        kw = {}
        reads = [in_]
        writes = [out]
        if bias is not None:
            if isinstance(bias, V):
                reads.append(bias)
                kw["bias"] = bias.ap
            else:
                kw["bias"] = bias
        if scale is not None:
            if isinstance(scale, V):
                reads.append(scale)
                kw["scale"] = scale.ap
            else:
                kw["scale"] = scale
        if accum is not None:
            writes.append(accum)
            kw["accum_out"] = accum.ap
        self.op("act", lambda e: e.activation(out=out.ap, in_=in_.ap, func=func, **kw), reads=reads, writes=writes)

    def tt(self, eng, out, in0, in1, op):
        self.op(eng, lambda e: e.tensor_tensor(out=out.ap, in0=in0.ap, in1=in1.ap, op=op),
                reads=[in0, in1], writes=[out])

    def ts(self, eng, out, in0, s1, s2=None, op0=ALU.mult, op1=None, accum=None):
        reads = [in0]
        writes = [out]
        a1 = s1
        a2 = s2
        if isinstance(s1, V):
            reads.append(s1)
            a1 = s1.ap
        if isinstance(s2, V):
            reads.append(s2)
            a2 = s2.ap
        kw = {}
        if op1 is not None:
            kw["op1"] = op1
        if accum is not None:
            writes.append(accum)
            kw["accum_out"] = accum.ap
        self.op(eng, lambda e: e.tensor_scalar(out=out.ap, in0=in0.ap, scalar1=a1, scalar2=a2, op0=op0, **kw),
                reads=reads, writes=writes)

    def stt(self, out, in0, scalar, in1, op0, op1, eng="dve"):
        reads = [in0, in1]
        a = scalar
        if isinstance(scalar, V):
            reads.append(scalar)
            a = scalar.ap
        self.op(eng, lambda e: e.scalar_tensor_tensor(out=out.ap, in0=in0.ap, scalar=a, in1=in1.ap, op0=op0, op1=op1),
                reads=reads, writes=[out])

    def copy(self, eng, out, in_):
        if eng == "act":
            self.op("act", lambda e: e.copy(out=out.ap, in_=in_.ap), reads=[in_], writes=[out])
        else:
            self.op(eng, lambda e: e.tensor_copy(out=out.ap, in_=in_.ap), reads=[in_], writes=[out])

    def memset(self, eng, out, val):
        self.op(eng, lambda e: e.memset(out.ap, val), writes=[out])

    def reduce(self, out, in_, op, axis=AX.X, eng="dve"):
        self.op(eng, lambda e: e.tensor_reduce(out=out.ap, in_=in_.ap, axis=axis, op=op), reads=[in_], writes=[out])

    def recip(self, out, in_, eng="dve"):
        self.op(eng, lambda e: e.reciprocal(out=out.ap, in_=in_.ap), reads=[in_], writes=[out])


D = 1024
N_META = 16
NH_SSM = 16
HD = 64
D_SSM = 1024
NG = 2
D_STATE = 128
D_CONV = 5
D_CONV_CH = 1536
MLA_H = 8
DN = 64
DR = 32
DV = 64
QLORA = 384
KVLORA = 256
D_ATTN = 512
D_MIX = 1536
SPLIT_Z = 1024
SPLIT_XBC = 2560
SPLIT_DT = 2592
SPLIT_CQ = 2976
SPLIT_CKV = 3232
D_IN = 3264
PEER_H = 8
NKEYS = 128
TOPK = 16
ALPHA = 2.0 ** 0.25
EPS = 1e-5
ATT_SCALE = 1.0 / (96.0 ** 0.5)


def build(S, NSEQ, dbg=()):
    NT = S // 128
    NTT = NT + 1
    LP = NTT * 128
    QG = min(512, S)
    NQG = S // QG
    nc = bass.Bass("TRN2", target_bir_lowering=False)
    K = KB(nc)

    def din(name, shape, dt=F32):
        return nc.dram_tensor(name, list(shape), dt, kind="ExternalInput").ap()

    x_d = din("x", [NSEQ, S, D])
    y_d = nc.dram_tensor("y", [NSEQ, S, D], F32, kind="ExternalOutput").ap()
    meta_d = din("meta", [N_META, D])
    vecs_d = din("vecs", [8, D])
    w_in_d = din("w_in", [D, D_IN])
    w_uq_d = din("w_uq", [QLORA, 768])
    w_ukv_d = din("w_ukv", [KVLORA, 1024])
    qg_d = din("q_norm_g", [128, 3])
    kvg_d = din("kv_norm_g", [128, 2])
    cs_d = din("cossin", [LP, 32])
    ident_d = din("ident", [128, 128])
    pad_d = din("padmask", [128, 1])
    cw_d = din("conv_w", [128, 12, 5])
    cb_d = din("conv_b", [128, 12])
    dtb_d = din("dt_bias", [1, 32])
    alog_d = din("a_log", [1, 32])
    dsk_d = din("d_skip", [1, 16])
    le_d = din("mask_le", [128, 128])
    ge_d = din("mask_ge", [128, 128])
    w_out_d = din("w_out", [D_MIX, D])
    ng_d = din("norm_g", [128, 12])
    wq_d = din("peer_wq", [D, 2048])
    skT_d = din("peer_skT", [128, 16, 128])
    uT_d = din("peer_uT", [128 * 128, 1024])
    v_d = din("peer_v", [128 * 128, 1024])
    iota_d = din("iota", [128, 128])
    uT_s = nc.dram_tensor("uT_s", [128 * 128, 1024], BF16, kind="Internal").ap()
    v_s = nc.dram_tensor("v_s", [128 * 128, 1024], BF16, kind="Internal").ap()
    uv_r = T(None, "uv_s", 2)
    xs_s = nc.dram_tensor("xs_s", [NTT, 128, 1024], BF16, kind="Internal").ap()
    sz_s = nc.dram_tensor("sz_s", [NT, 128, 1024], BF16, kind="Internal").ap()
    sinb_s = nc.dram_tensor("sinb_s", [NTT, 128, 1024], BF16, kind="Internal").ap()
    h1_s = nc.dram_tensor("h1_s", [NSEQ, S, D], F32, kind="Internal").ap()
    xs_r = T(None, "xs_s", NTT)
    sz_r = T(None, "sz_s", NT)
    sinb_r = T(None, "sinb_s", NTT)
    h1_r = T(None, "h1_s", NSEQ * NT)
    dbg_out = {}

    def dump(name, v, shape, dt=F32):
        if name not in dbg:
            return
        d = nc.dram_tensor("dbg_" + name, list(shape), dt, kind="ExternalOutput").ap()
        dbg_out[name] = d
        K.dma("sp", d, v, sem="dbg")

    st = K.stack
    with st:
        ident = K.sb("ident", [128, 128], F32)
        K.dma("sp", ident[:], ident_d)
        padm = K.sb("padm", [128, 1], F32)
        K.dma("sp", padm[:], pad_d)
        cast_state = [0]
        cst_box = [None]

        def issue_casts(n):
            if cst_box[0] is None:
                return
            for _ in range(n):
                i = cast_state[0]
                if i >= 64:
                    return
                cast_state[0] += 1
                r0 = (i // 2) * 512
                stg_ = cst_box[0][i % 2]
                src, dst, reg = (uT_d, uT_s, uv_r.s(0)) if i % 2 == 0 else (v_d, v_s, uv_r.s(1))
                K.dma("pool", stg_[:], src[r0:r0 + 512, :].rearrange("(j p) c -> p j c", p=128), sem="cst_l" + str(i % 2))
                K.dma("sp", dst[r0:r0 + 512, :].rearrange("(j p) c -> p j c", p=128), stg_[:], writes=[reg], sem="cst_s" + str(i % 2))
        ones = K.sb("ones", [128, 128], F32)
        K.memset("pool", ones[:], 1.0)
        cs_all = K.sb("cs_all", [128, NTT, 32], F32)
        K.dma("sp", cs_all[:], cs_d.rearrange("(t p) c -> p t c", p=128))

        def alloc_ln(stack, i0, nb=2):
            vecs_box[0] = K.sb("vecs", [128, 2, D], F32, stack=stack)
            for i in range(2):
                K.dma("sp", vecs_box[0][:, i, :], vecs_d[i0 + i:i0 + i + 1, :].to_broadcast([128, D]))
            for i in range(nb):
                xb[i] = K.sb(f"xb{i}", [128, D], F32, stack=stack)
                xn[i] = K.sb(f"xn{i}", [128, D], F32, stack=stack)
        PS = [K.psum(f"ps{i}", [128, 1024], F32, nreg=2) for i in range(4)]

        def bank(i, b, n=512):
            return PS[i].s(b, (slice(None), slice(b * 512, b * 512 + n)))
        mle = K.sb("mle", [128, 128], F32)
        mge = K.sb("mge", [128, 128], F32)
        K.dma("sp", mle[:], le_d)
        K.dma("sp", mge[:], ge_d)
        mgt = K.sb("mgt", [128, 128], BF16)
        mlt = K.sb("mlt", [128, 128], BF16)
        K.ts("dve", mgt[:], mle[:], -1.0, 1.0, op0=ALU.mult, op1=ALU.add)
        K.ts("dve", mlt[:], mge[:], -1.0, 1.0, op0=ALU.mult, op1=ALU.add)
        dsk = K.sb("dsk", [128, 16], F32)
        K.dma("sp", dsk[:], dsk_d.to_broadcast([128, 16]))
        xb = [None, None]
        xn = [None, None]
        vecs_box = [None]
        stt = K.sb("stt", [128, 2, 6], F32)
        mv = K.sb("mv", [128, 2], F32)
        rs = K.sb("rs", [128, 1], F32)

        def ln_tile(src, dst, gi, bi):
            K.op("dve", lambda e: e.bn_stats(out=stt.h[:, 0, :], in_=src.ap[:, 0:512]), reads=[src], writes=[stt[:]])
            K.op("dve", lambda e: e.bn_stats(out=stt.h[:, 1, :], in_=src.ap[:, 512:1024]), reads=[src], writes=[stt[:]])
            K.op("dve", lambda e: e.bn_aggr(out=mv.h[:], in_=stt.h[:].rearrange("p a b -> p (a b)")), reads=[stt[:]], writes=[mv[:]])
            K.ts("dve", rs[:], mv[:, 1:2], EPS, None, op0=ALU.add)
            K.act(rs[:], rs[:], AF.Sqrt)
            K.recip(rs[:], rs[:])
            K.ts("dve", dst, src, mv[:, 0:1], rs[:], op0=ALU.subtract, op1=ALU.mult)
            K.tt("dve", dst, dst, vecs_box[0][:, gi, :], ALU.mult)
            K.tt("pool", dst, dst, vecs_box[0][:, bi, :], ALU.add)

        def load_x(seq, t, buf):
            if t == 0:
                K.memset("pool", buf[:], 0.0)
                K.dma("sp", buf[112:128, :], meta_d, sem="x" + buf.name)
            else:
                K.dma("sp", buf[:], x_d[seq, (t - 1) * 128:t * 128, :], sem="x" + buf.name)

        for seq in range(NSEQ):
            with ExitStack() as s_seq:
                if seq == 0:
                    cst_box[0] = [K.sb(f"cst{i}", [128, 4, 1024], BF16, stack=s_seq) for i in range(2)]
                yaT = K.sb("yaT", [128, 4, S], BF16, stack=s_seq)
                hT = K.sb("hT", [128, 8, LP], BF16, stack=s_seq)
                s1 = ExitStack()
                alloc_ln(s1, 0)
                for t in range(NTT):
                    buf = xb[t % 2]
                    dst = xn[t % 2]
                    load_x(seq, t, buf)
                    ln_tile(buf[:], dst[:], 0, 1)
                    for half in range(2):
                        pb = PS[half]
                        for j in range(4):
                            kc = half * 4 + j
                            K.tr(pb[:, j * 128:(j + 1) * 128], dst[:, kc * 128:(kc + 1) * 128], ident[:])
                        K.copy("act", hT[:, half * 4:half * 4 + 4, t * 128:(t + 1) * 128],
                               pb[:, 0:512].re("p (a b) -> p a b", a=4))
                if seq == 0:
                    dump("hT", hT[:], [128, 8, LP], BF16)
                K.flush("p1_ln")
                s1.close()
                with ExitStack() as s2:
                    wm = K.sb("wm", [128, 8, 672], BF16, stack=s2)
                    K.dma("pool", wm[:], w_in_d[:, SPLIT_DT:D_IN].rearrange("(kc p) c -> p kc c", p=128))
                    wuq = K.sb("wuq", [128, 3, 768], BF16, stack=s2)
                    wukv = K.sb("wukv", [128, 2, 1024], BF16, stack=s2)
                    with ExitStack() as s_tmp:
                        wf = K.sb("wf", [128, 3, 768], F32, stack=s_tmp)
                        wf2 = K.sb("wf2", [128, 2, 1024], F32, stack=s_tmp)
                        gq = K.sb("gq", [128, 3], F32, stack=s_tmp)
                        gkv = K.sb("gkv", [128, 2], F32, stack=s_tmp)
                        K.dma("sp", wf[:], w_uq_d.rearrange("(kc p) c -> p kc c", p=128))
                        K.dma("sp", wf2[:], w_ukv_d.rearrange("(kc p) c -> p kc c", p=128))
                        K.dma("sp", gq[:], qg_d)
                        K.dma("sp", gkv[:], kvg_d)
                        for fc in range(3):
                            K.ts("dve", wuq[:, fc, :], wf[:, fc, :], gq[:, fc:fc + 1], None, op0=ALU.mult)
                        for fc in range(2):
                            K.ts("dve", wukv[:, fc, :], wf2[:, fc, :], gkv[:, fc:fc + 1], None, op0=ALU.mult)
                        K.flush("p2w")

                    QT = K.sb("QT", [128, 8, S], BF16, stack=s2)
                    KT = K.sb("KT", [128, 8, LP], BF16, stack=s2)
                    Vaug = K.sb("Vaug", [128, NTT, 8, 65], BF16, stack=s2)
                    junk = K.sb("junk", [128, 512], F32, stack=s2)
                    ssq = K.sb("ssq", [128, 2], F32, stack=s2)
                    r2 = K.sb("r2", [128, 2], F32, stack=s2)
                    cqn = K.sb("cqn", [128, 640], F32, stack=s2)
                    cT = K.sb("cT", [128, 5, 128], BF16, stack=s2)
                    Qcat = K.sb("Qcat", [128, 8, 96], F32, stack=s2)
                    Kcat = K.sb("Kcat", [128, 8, 96], F32, stack=s2)
                    tmpr = K.sb("tmpr", [128, 8, 16], F32, stack=s2)
                    krr = K.sb("krr", [128, 32], F32, stack=s2)
                    sq = K.sb("sq", [128, 8, 96], F32, stack=s2)
                    n2 = K.sb("n2", [128, 16], F32, stack=s2)
                    qk2 = K.sb("qk2", [128, 16], F32, stack=s2)
                    K.memset("pool", qk2[:], 0.0)
                    K.memset("pool", Vaug[:, :, :, 64:65], 1.0)
                    for t in range(NTT):
                        issue_casts(2)
                        tok = slice(t * 128, (t + 1) * 128)
                        PA = bank(0, 0, 384)
                        PB = bank(0, 1, 288)
                        for kc in range(8):
                            K.mm(PA, hT[:, kc, tok], wm[:, kc, 0:384], start=(kc == 0), stop=(kc == 7))
                        for kc in range(8):
                            K.mm(PB, hT[:, kc, tok], wm[:, kc, 384:672], start=(kc == 0), stop=(kc == 7))
                        K.memset("pool", ssq[:], 0.0)
                        K.act(junk[:, 0:384], PA, AF.Square, accum=ssq[:, 0:1])
                        K.act(junk[:, 0:256], PB[:, 0:256], AF.Square, accum=ssq[:, 1:2])
                        K.ts("dve", r2[:, 0:1], ssq[:, 0:1], 1.0 / 384, EPS, op0=ALU.mult, op1=ALU.add)
                        K.ts("dve", r2[:, 1:2], ssq[:, 1:2], 1.0 / 256, EPS, op0=ALU.mult, op1=ALU.add)
                        K.act(r2[:], r2[:], AF.Sqrt)
                        K.recip(r2[:], r2[:])
                        K.ts("dve", cqn[:, 0:384], PA, r2[:, 0:1], None, op0=ALU.mult)
                        K.ts("dve", cqn[:, 384:640], PB[:, 0:256], r2[:, 1:2], None, op0=ALU.mult)
                        cos = cs_all[:, t, 0:16]
                        sin = cs_all[:, t, 16:32]
                        K.tt("dve", krr[:, 0:16], PB[:, 256:272], cos, ALU.mult)
                        K.tt("dve", tmpr[:, 0, :], PB[:, 272:288], sin, ALU.mult)
                        K.tt("dve", krr[:, 0:16], krr[:, 0:16], tmpr[:, 0, :], ALU.subtract)
                        K.tt("dve", krr[:, 16:32], PB[:, 256:272], sin, ALU.mult)
                        K.tt("dve", tmpr[:, 0, :], PB[:, 272:288], cos, ALU.mult)
                        K.tt("dve", krr[:, 16:32], krr[:, 16:32], tmpr[:, 0, :], ALU.add)
                        for j in range(5):
                            b = 0 if j < 4 else 1
                            K.tr(PS[1].s(b, (slice(None), slice(j * 128, (j + 1) * 128))), cqn[:, j * 128:(j + 1) * 128], ident[:])
                        K.copy("act", cT[:], PS[1][:, 0:640].re("p (a b) -> p a b", a=5))
                        if t >= 1:
                            for fc in range(3):
                                K.mm(bank(2, 0), cT[:, fc, :], wuq[:, fc, 0:512], start=(fc == 0), stop=(fc == 2))
                            for fc in range(3):
                                K.mm(bank(2, 1, 256), cT[:, fc, :], wuq[:, fc, 512:768], start=(fc == 0), stop=(fc == 2))
                        for fc in range(2):
                            K.mm(bank(3, 0), cT[:, 3 + fc, :], wukv[:, fc, 0:512], start=(fc == 0), stop=(fc == 1))
                        for fc in range(2):
                            K.mm(bank(3, 1), cT[:, 3 + fc, :], wukv[:, fc, 512:1024], start=(fc == 0), stop=(fc == 1))
                        kv3 = PS[3][:, 0:1024].re("p (h c) -> p h c", h=8)
                        K.copy("act", Kcat[:, :, 0:64], kv3[:, :, 0:64])
                        K.copy("pool", Kcat[:, :, 64:96], krr[:, :].bc(1, [128, 8, 32]))
                        K.copy("act", Vaug[:, t, :, 0:64], kv3[:, :, 64:128])
                        if t == 0:
                            K.ts("dve", Vaug[:, 0, :, :], Vaug[:, 0, :, :], padm[:, 0:1], None, op0=ALU.mult)
                        K.tt("pool", sq[:], Kcat[:], Kcat[:], ALU.mult)
                        K.reduce(n2[:, 8:16], sq[:], ALU.add)
                        K.tt("dve", qk2[:, 8:16], qk2[:, 8:16], n2[:, 8:16], ALU.max)
                        for h in range(8):
                            K.tr(PS[0].s(h // 4, (slice(0, 96), slice(h * 128, (h + 1) * 128))), Kcat[:, h, :], ident[:])
                        K.copy("act", KT[0:96, :, tok], PS[0][0:96, :].re("p (h c) -> p h c", h=8))
                        if t >= 1:
                            q3 = PS[2][:, 0:768].re("p (h c) -> p h c", h=8)
                            cosb = cos.bc(1, [128, 8, 16])
                            sinb = sin.bc(1, [128, 8, 16])
                            K.copy("act", Qcat[:, :, 0:64], q3[:, :, 0:64])
                            K.tt("dve", Qcat[:, :, 64:80], q3[:, :, 64:80], cosb, ALU.mult)
                            K.tt("dve", tmpr[:], q3[:, :, 80:96], sinb, ALU.mult)
                            K.tt("dve", Qcat[:, :, 64:80], Qcat[:, :, 64:80], tmpr[:], ALU.subtract)
                            K.tt("dve", Qcat[:, :, 80:96], q3[:, :, 64:80], sinb, ALU.mult)
                            K.tt("dve", tmpr[:], q3[:, :, 80:96], cosb, ALU.mult)
                            K.tt("dve", Qcat[:, :, 80:96], Qcat[:, :, 80:96], tmpr[:], ALU.add)
                            K.tt("pool", sq[:], Qcat[:], Qcat[:], ALU.mult)
                            K.reduce(n2[:, 0:8], sq[:], ALU.add)
                            K.tt("dve", qk2[:, 0:8], qk2[:, 0:8], n2[:, 0:8], ALU.max)
                            for h in range(8):
                                K.tr(PS[1].s(h // 4, (slice(0, 96), slice(h * 128, (h + 1) * 128))), Qcat[:, h, :], ident[:])
                            K.copy("act", QT[0:96, :, (t - 1) * 128:t * 128], PS[1][0:96, :].re("p (h c) -> p h c", h=8))
                    m16 = K.sb("m16", [16, 1], F32, stack=s2)
                    dg = K.sb("dg", [16, 16], F32, stack=s2)
                    bq = K.sb("bq", [128, 16], F32, stack=s2)
                    bb = K.sb("bb", [128, 8], F32, stack=s2)
                    K.tr(bank(0, 0)[0:16, 0:128], qk2[:, 0:16], ident[:])
                    K.reduce(m16[:], bank(0, 0)[0:16, 0:128], ALU.max)
                    K.ts("dve", dg[:], ident[0:16, 0:16], m16[:, 0:1], None, op0=ALU.mult)
                    K.mm(bank(0, 1)[:, 0:16], ones[0:16, :], dg[:])
                    K.copy("dve", bq[:], bank(0, 1)[:, 0:16])
                    K.tt("dve", bb[:], bq[:, 0:8], bq[:, 8:16], ALU.mult)
                    K.act(bb[:], bb[:], AF.Sqrt)
                    K.ts("dve", bb[:], bb[:], -ATT_SCALE, None, op0=ALU.mult)
                    NB = QG // 128
                    PT = [K.sb(f"PT{i}", [128, QG], BF16, stack=s2) for i in range(4)]
                    OT = [K.sb(f"OT{i}", [65, QG], F32, stack=s2) for i in range(2)]
                    Otok = K.sb("Otok", [128, NB, 512], F32, stack=s2)
                    rsum = K.sb("rsum", [128, NB], F32, stack=s2)
                    ss = K.sb("ss", [128, NB], F32, stack=s2)
                    sidx = 0
                    for qg in range(NQG):
                        qs = slice(qg * QG, (qg + 1) * QG)
                        items = [(h, kt) for h in range(8) for kt in range(NTT)]
                        LAG = 2
                        pend = []

                        def emit_pv(h, kt, pt):
                            PO = bank(3, h % 2, QG)[0:65, :]
                            K.mm(PO, Vaug[:, kt, h, :], pt[:], start=(kt == 0), stop=(kt == NTT - 1))
                            if kt == NTT - 1:
                                ot = OT[h % 2]
                                K.copy("dve", ot[:], PO)
                                PTr = bank(2, h % 2, NB * 65)
                                for b in range(NB):
                                    K.tr(PTr[:, b * 65:(b + 1) * 65], ot[0:65, b * 128:(b + 1) * 128], ident[0:65, 0:65])
                                p3 = PTr.re("p (b c) -> p b c", c=65)
                                K.recip(rsum[:], p3[:, :, 64])
                                K.tt("dve", Otok[:, :, h * 64:(h + 1) * 64], p3[:, :, 0:64], rsum[:].bc(2, [128, NB, 64]), ALU.mult)

                        for (h, kt) in items:
                            PSs = bank(sidx % 2, (sidx // 2) % 2, QG)
                            pt = PT[sidx % 4]
                            sidx += 1
                            K.mm(PSs, KT[0:96, h, kt * 128:(kt + 1) * 128], QT[0:96, h, qs])
                            K.act(pt[:], PSs, AF.Exp, scale=ATT_SCALE, bias=bb[:, h:h + 1])
                            pend.append((h, kt, pt))
                            if len(pend) > LAG:
                                emit_pv(*pend.pop(0))
                        while pend:
                            emit_pv(*pend.pop(0))
                        K.memset("pool", ss[:], 0.0)
                        for b in range(NB):
                            K.act(junk[:], Otok[:, b, :], AF.Square, accum=ss[:, b:b + 1])
                        K.ts("dve", ss[:], ss[:], 1.0 / 512, EPS, op0=ALU.mult, op1=ALU.add)
                        K.act(ss[:], ss[:], AF.Sqrt)
                        K.recip(ss[:], ss[:])
                        for b in range(NB):
                            K.ts("dve", Otok[:, b, :], Otok[:, b, :], ss[:, b:b + 1], None, op0=ALU.mult)
                            pb = bank(2, b % 2)
                            for fc in range(4):
                                K.tr(pb[:, fc * 128:(fc + 1) * 128], Otok[:, b, fc * 128:(fc + 1) * 128], ident[:])
                            t0 = qg * QG + b * 128
                            K.copy("act", yaT[:, :, t0:t0 + 128], pb.re("p (a b) -> p a b", a=4))
                    if seq == 0:
                        dump("yaT", yaT[:], [128, 4, S], BF16)
                        dump("QT", QT[0:96], [96, 8, S], BF16)
                        dump("KT", KT[0:96], [96, 8, LP], BF16)
                        dump("Vaug", Vaug[:], [128, NTT, 8, 65], BF16)
                    K.flush("p2_mla")

                s3 = ExitStack()
                BT = K.sb("BT", [128, 2, LP], BF16, stack=s3)
                CT = K.sb("CT", [128, 2, LP], BF16, stack=s3)
                Btok = K.sb("Btok", [128, NTT, 2, 128], BF16, stack=s3)
                DTa = K.sb("DTa", [128, NTT, 32], F32, stack=s3)
                YS = K.sb("YS", [128, NTT, 32], F32, stack=s3)
                WX = K.sb("WX", [128, NTT, 32], F32, stack=s3)
                CD = K.sb("CD", [128, NTT, 32], F32, stack=s3)
                AAb = K.sb("AAb", [128, NTT, 32], BF16, stack=s3)
                with ExitStack() as s3a:
                    wx = K.sb("wx", [128, 8, 1568], BF16, stack=s3a)
                    K.dma("pool", wx[:], w_in_d[:, SPLIT_Z:SPLIT_DT].rearrange("(kc p) c -> p kc c", p=128))
                    cw = K.sb("cw", [128, 12, 5], F32, stack=s3a)
                    cb = K.sb("cb", [128, 12], F32, stack=s3a)
                    K.dma("sp", cw[:], cw_d)
                    K.dma("sp", cb[:], cb_d)
                    xpres = [K.sb(f"xpre{i}", [128, LP + 4], F32, stack=s3a) for i in range(2)]
                    accs = [K.sb(f"acc{i}", [128, LP], F32, stack=s3a) for i in range(2)]
                    sils = [K.sb(f"sil{i}", [128, LP], F32, stack=s3a) for i in range(2)]
                    stg = [K.sb(f"stg{i}", [128, 4, 128], BF16, stack=s3a) for i in range(2)]
                    for xp_ in xpres:
                        K.memset("pool", xp_[:], 0.0)
                    groups = [(g0, min(g0 + 512, LP)) for g0 in range(0, LP, 512)]
                    pidx = 0
                    def conv_stage1(c):
                        nonlocal_p[0] = nonlocal_p[0]
                        issue_casts(2)
                        xpre = xpres[c % 2]
                        for (g0, g1) in groups:
                            pb = bank(nonlocal_p[0] % 4, (nonlocal_p[0] // 4) % 2, g1 - g0)
                            nonlocal_p[0] += 1
                            for kc in range(8):
                                K.mm(pb, wx[:, kc, c * 128:(c + 1) * 128], hT[:, kc, g0:g1], start=(kc == 0), stop=(kc == 7))
                            K.copy("act", xpre[:, 2 + g0:2 + g1], pb)
                        K.memset("pool", xpre[:, 2:114], 0.0)

                    def conv_stage2(c):
                        xpre = xpres[c % 2]
                        acc = accs[c % 2]
                        sil = sils[c % 2]
                        K.ts("dve", acc[:], xpre[:, 0:LP], cw[:, c, 0:1], None, op0=ALU.mult)
                        for k in range(1, 5):
                            K.stt(acc[:], xpre[:, k:k + LP], cw[:, c, k:k + 1], acc[:], ALU.mult, ALU.add)
                        K.act(sil[:], acc[:], AF.Silu, bias=cb[:, c:c + 1])

                    def conv_stage3(c):
                        sil = sils[c % 2]
                        if c >= 10:
                            K.copy("pool", CT[:, c - 10, :], sil[:])
                            return
                        if c >= 8:
                            K.copy("pool", BT[:, c - 8, :], sil[:])
                        for t0 in range(0, NTT, 4):
                            n = min(4, NTT - t0)
                            pb = bank(nonlocal_p[0] % 4, (nonlocal_p[0] // 4) % 2, n * 128)
                            nonlocal_p[0] += 1
                            for j in range(n):
                                K.tr(pb[:, j * 128:(j + 1) * 128], sil[:, (t0 + j) * 128:(t0 + j + 1) * 128], ident[:])
                            if c >= 8:
                                K.copy("act", Btok[:, t0:t0 + n, c - 8, :], pb.re("p (a b) -> p a b", b=128))
                            else:
                                sg = stg[(t0 // 4) % 2]
                                K.copy("act", sg[:, 0:n, :], pb.re("p (a b) -> p a b", b=128))
                                K.dma("sp", xs_s[t0:t0 + n, :, c * 128:(c + 1) * 128].rearrange("t p c -> p t c"), sg[:, 0:n, :],
                                      writes=[xs_r.ss(range(t0, t0 + n))], sem="xsw" + sg.name)

                    nonlocal_p = [pidx]
                    conv_stage1(0)
                    for c in range(12):
                        if c + 1 < 12:
                            conv_stage1(c + 1)
                        conv_stage2(c)
                        conv_stage3(c)
                    dtb = K.sb("dtb", [128, 32], F32, stack=s3a)
                    aneg = K.sb("aneg", [128, 32], F32, stack=s3a)
                    K.dma("sp", dtb[:], dtb_d.to_broadcast([128, 32]))
                    K.dma("sp", aneg[:], alog_d.to_broadcast([128, 32]))
                    K.act(aneg[:], aneg[:], AF.Exp)
                    K.ts("dve", aneg[:], aneg[:], -1.0, None, op0=ALU.mult)
                    AA = K.sb("AA", [128, NTT, 32], F32, stack=s3a)
                    X1 = K.sb("X1", [128, NTT, 32], F32, stack=s3a)
                    X2 = K.sb("X2", [128, NTT, 32], F32, stack=s3a)
                    TOT = K.sb("TOT", [128, NTT, 32], F32, stack=s3a)
                    W = NTT * 32
                    for t in range(NTT):
                        pb = PS[0].s(t // 16, (slice(None), slice(t * 32, (t + 1) * 32)))
                        for kc in range(8):
                            K.mm(pb, hT[:, kc, t * 128:(t + 1) * 128], wx[:, kc, 1536:1568], start=(kc == 0), stop=(kc == 7))
                    p3 = PS[0][:, 0:W].re("p (t c) -> p t c", c=32)
                    K.tt("dve", X1[:], p3, dtb[:].bc(1, [128, NTT, 32]), ALU.add)
                    K.act(X1[:], X1[:], AF.Exp)
                    K.act(DTa[:], X1[:], AF.Ln, bias=1.0)
                    K.ts("dve", DTa[:, 0, :], DTa[:, 0, :], padm[:, 0:1], None, op0=ALU.mult)
                    K.tt("dve", AA[:], DTa[:], aneg[:].bc(1, [128, NTT, 32]), ALU.mult)
                    K.copy("pool", AAb[:], AA[:])
                    for t in range(NTT):
                        K.mm(PS[1].s(t // 16, (slice(None), slice(t * 32, (t + 1) * 32))), mle[:], AA[:, t, :])
                        K.mm(PS[2].s(t // 16, (slice(None), slice(t * 32, (t + 1) * 32))), ones[:], AA[:, t, :])
                    K.copy("dve", X1[:], PS[1][:, 0:W].re("p (t c) -> p t c", c=32))
                    K.copy("act", TOT[:], PS[2][:, 0:W].re("p (t c) -> p t c", c=32))
                    K.tt("dve", X1[:, :, 16:32], X1[:, :, 16:32], AA[:, :, 16:32], ALU.subtract)
                    K.tt("dve", X2[:], TOT[:], X1[:], ALU.subtract)
                    K.act(X1[:], X1[:], AF.Exp)
                    K.act(X2[:], X2[:], AF.Exp)
                    K.act(CD[:], TOT[:], AF.Exp)
                    K.copy("pool", YS[:, :, 0:16], X1[:, :, 0:16])
                    K.copy("pool", YS[:, :, 16:32], X2[:, :, 16:32])
                    K.tt("dve", WX[:, :, 0:16], DTa[:, :, 0:16], X2[:, :, 0:16], ALU.mult)
                    K.tt("dve", WX[:, :, 16:32], DTa[:, :, 16:32], X1[:, :, 16:32], ALU.mult)
                    if seq == 0:
                        dump("BT", BT[:], [128, 2, LP], BF16)
                        dump("CT", CT[:], [128, 2, LP], BF16)
                        dump("Btok", Btok[:], [128, NTT, 2, 128], BF16)
                        dump("DTa", DTa[:], [128, NTT, 32])
                        dump("YS", YS[:], [128, NTT, 32])
                        dump("WX", WX[:], [128, NTT, 32])
                        dump("CD", CD[:], [128, NTT, 32])
                    K.flush("p3a_conv")
                with ExitStack() as s3c:
                    wz = K.sb("wz", [128, 8, 1024], BF16, stack=s3c)
                    K.dma("pool", wz[:], w_in_d[:, 0:SPLIT_Z].rearrange("(kc p) c -> p kc c", p=128))
                    zs = [K.sb(f"zs{i}", [128, 1024], BF16, stack=s3c) for i in range(2)]
                    for t in range(1, NTT):
                        issue_casts(2)
                        for half in range(2):
                            pb = bank(t % 2, half)
                            for kc in range(8):
                                K.mm(pb, hT[:, kc, t * 128:(t + 1) * 128], wz[:, kc, half * 512:(half + 1) * 512], start=(kc == 0), stop=(kc == 7))
                        z = zs[t % 2]
                        K.act(z[:], PS[t % 2][:, :], AF.Silu)
                        K.dma("sp", sz_s[t - 1], z[:], writes=[sz_r.s(t - 1)], sem="szw" + z.name)
                    K.flush("p3c_z")
                if seq == 0 and "xs" in dbg:
                    with ExitStack() as sd:
                        tmpx = K.sb("tmpx", [128, NTT, 1024], BF16, stack=sd)
                        K.dma("sp", tmpx[:], xs_s.rearrange("t p c -> p t c"), reads=[xs_r[:]] if False else [V(None, xs_r.regs)], sem="dbg")
                        dump("xs", tmpx[:], [128, NTT, 1024], BF16)
                        tmpz = K.sb("tmpz", [128, NT, 1024], BF16, stack=sd)
                        K.dma("sp", tmpz[:], sz_s.rearrange("t p c -> p t c"), reads=[V(None, sz_r.regs)], sem="dbg")
                        dump("sz", tmpz[:], [128, NT, 1024], BF16)
                        K.flush("dbgxs")

                with ExitStack() as s4:
                    xsb = [K.sb(f"xsb{i}", [128, 1024], BF16, stack=s4) for i in range(2)]
                    xw = [K.sb(f"xw{i}", [128, 1024], BF16, stack=s4) for i in range(2)]
                    Sb = K.sb("Sb", [128, 1024], F32, stack=s4)
                    tmpS = K.sb("tmpS", [128, 1024], F32, stack=s4)
                    sst = [K.sb(f"sst{i}", [128, 1024], BF16, stack=s4) for i in range(2)]
                    K.memset("pool", Sb[:], 0.0)
                    K.memset("pool", sst[0][:], 0.0)
                    K.dma("sp", sinb_s[NTT - 1], sst[0][:], writes=[sinb_r.s(NTT - 1)], sem="sbw0")
                    it = 0
                    for c in range(NTT - 1, 1, -1):
                        it += 1
                        xs_t = xsb[it % 2]
                        K.dma("sp", xs_t[:], xs_s[c], reads=[xs_r.s(c)], sem="xsl" + xs_t.name)
                        w = xw[it % 2]
                        K.tt("dve", w[:].re("p (h d) -> p h d", d=64), xs_t[:].re("p (h d) -> p h d", d=64),
                             WX[:, c, 16:32].bc(2, [128, 16, 64]), ALU.mult)
                        for g in range(2):
                            K.mm(bank(it % 2, g), Btok[:, c, g, :], w[:, g * 512:(g + 1) * 512])
                        K.tt("dve", tmpS[:].re("p (h d) -> p h d", d=64), Sb[:].re("p (h d) -> p h d", d=64),
                             CD[:, c, 16:32].bc(2, [128, 16, 64]), ALU.mult)
                        K.tt("dve", Sb[:], tmpS[:], PS[it % 2][:, :], ALU.add)
                        so = sst[it % 2]
                        K.copy("act", so[:], Sb[:])
                        K.dma("sp", sinb_s[c - 1], so[:], writes=[sinb_r.s(c - 1)], sem="sbw" + str(it % 2))
                    K.flush("p4_bwd")
                with ExitStack() as s5:
                    vecs4 = K.sb("vecs4", [128, 4, D], F32, stack=s5)
                    for i in range(4):
                        K.dma("sp", vecs4[:, i, :], vecs_d[i:i + 1, :].to_broadcast([128, D]))
                    vecs_box[0] = vecs4
                    xb[0] = K.sb("xb5", [128, D], F32, stack=s5)
                    hres = xb[0]
                    wo = K.sb("wo", [128, 12, D], BF16, stack=s5)
                    with ExitStack() as s5w:
                        wof = K.sb("wof", [128, 12, D], F32, stack=s5w)
                        ngs = K.sb("ngs", [128, 12], F32, stack=s5w)
                        K.dma("sp", wof[:], w_out_d.rearrange("(kc p) c -> p kc c", p=128))
                        K.dma("sp", ngs[:], ng_d)
                        for fc in range(12):
                            K.ts("dve" if fc % 2 else "pool", wo[:, fc, :], wof[:, fc, :], ngs[:, fc:fc + 1], None, op0=ALU.mult)
                        K.flush("p5w")
                    xsb = [K.sb(f"xsc{i}", [128, 1024], BF16, stack=s5) for i in range(2)]
                    szb = [K.sb(f"szb{i}", [128, 1024], BF16, stack=s5) for i in range(2)]
                    sbb = [K.sb(f"sbb{i}", [128, 1024], BF16, stack=s5) for i in range(2)]
                    xdt = [K.sb(f"xdt{i}", [128, 1024], BF16, stack=s5) for i in range(2)]
                    xwf = xdt[0]
                    Sf = K.sb("Sf", [128, 1024], F32, stack=s5)
                    Sfb = K.sb("Sfb", [128, 1024], BF16, stack=s5)
                    cbm = K.sb("cbm", [128, 2, 2, 128], BF16, stack=s5)
                    rhsD = [K.sb(f"rhsD{i}", [128, 16, 128], BF16, stack=s5) for i in range(2)]
                    Eb = [K.sb(f"Eb{i}", [128, 4, 128], BF16, stack=s5) for i in range(2)]
                    MT = K.sb("MT", [128, 2, 16, 128], BF16, stack=s5)
                    t1 = K.sb("t1", [128, 1024], F32, stack=s5)
                    t2 = K.sb("t2", [128, 1024], F32, stack=s5)
                    ssy = K.sb("ssy", [128, 1], F32, stack=s5)
                    ymT = K.sb("ymT", [128, 8, 128], BF16, stack=s5)
                    h1b = [K.sb("h1b0", [128, D], F32, stack=s5)] * 2
                    tmpS = t2
                    junk5 = t2
                    rr = t2
                    yb = t1
                    K.memset("pool", Sf[:], 0.0)
                    K.memset("pool", Sfb[:], 0.0)
                    r3 = lambda v: v.re("p (h d) -> p h d", d=64)
                    def p5_loads(c):
                        xs_t = xsb[c % 2]
                        K.dma("sp", xs_t[:], xs_s[c], reads=[xs_r.s(c)], sem="xsl" + xs_t.name)
                        if c >= 1:
                            sz_t = szb[c % 2]
                            K.dma("sp", sz_t[:], sz_s[c - 1], reads=[sz_r.s(c - 1)], sem="szl" + sz_t.name)
                            sb_t = sbb[c % 2]
                            K.dma("sp", sb_t[:], sinb_s[c], reads=[sinb_r.s(c)], sem="sbl" + sb_t.name)

                    def p5_A(c):
                        tok = slice(c * 128, (c + 1) * 128)
                        xs_t = xsb[c % 2]
                        for g in range(2):
                            K.mm(bank(1, 0)[:, g * 128:(g + 1) * 128], BT[:, g, tok], CT[:, g, tok])
                        cb3 = bank(1, 0)[:, 0:256].re("p (g l) -> p g l", g=2)
                        K.tt("dve", cbm[:, 0, :, :], cb3, mle[:].bc(1, [128, 2, 128]), ALU.mult)
                        K.tt("dve", cbm[:, 1, :, :], cb3, mge[:].bc(1, [128, 2, 128]), ALU.mult)
                        for d in range(2):
                            hs = slice(d * 16, (d + 1) * 16)
                            msk = mle if d == 0 else mge
                            lt = mgt if d == 0 else mlt
                            K.tt("dve", rhsD[d][:], AAb[:, c, hs].bc(2, [128, 16, 128]), msk[:].bc(1, [128, 16, 128]), ALU.mult)
                            K.tt("dve", r3(xdt[d][:]), r3(xs_t[:]), DTa[:, c, hs].bc(2, [128, 16, 64]), ALU.mult)
                            for q4 in range(4):
                                pb = bank(0, didx[0] % 2)
                                eb = Eb[didx[0] % 2]
                                didx[0] += 1
                                K.mm(pb, lt[:], rhsD[d][:, q4 * 4:(q4 + 1) * 4, :].re("p a b -> p (a b)"))
                                K.act(eb[:].re("p a b -> p (a b)"), pb, AF.Exp)
                                K.tt("dve", MT[:, d, q4 * 4:(q4 + 1) * 4, :], eb[:], cbm[:, d, q4 // 2, :].bc(1, [128, 4, 128]), ALU.mult)
                        for h in range(16):
                            yo_ = PS[2].s(h // 8, (slice(None), slice(h * 64, (h + 1) * 64)))
                            K.mm(yo_, MT[:, 0, h, :], xdt[0][:, h * 64:(h + 1) * 64], start=True, stop=False)
                            K.mm(yo_, MT[:, 1, h, :], xdt[1][:, h * 64:(h + 1) * 64], start=False, stop=True)

                    def p5_B1(c):
                        tok = slice(c * 128, (c + 1) * 128)
                        xs_t = xsb[c % 2]
                        sz_t = szb[c % 2]
                        sb_t = sbb[c % 2]
                        for g in range(2):
                            K.mm(bank(3, g), CT[:, g, tok], Sfb[:, g * 512:(g + 1) * 512])
                        K.tt("dve", r3(t1[:]), r3(PS[3][:, :]), YS[:, c, 0:16].bc(2, [128, 16, 64]), ALU.mult)
                        for g in range(2):
                            K.mm(bank(3, g), CT[:, g, tok], sb_t[:, g * 512:(g + 1) * 512])
                        K.tt("dve", r3(t2[:]), r3(PS[3][:, :]), YS[:, c, 16:32].bc(2, [128, 16, 64]), ALU.mult)
                        K.tt("pool", t1[:], t1[:], t2[:], ALU.add)
                        K.tt("dve", r3(t2[:]), r3(xs_t[:]), dsk[:].bc(2, [128, 16, 64]), ALU.mult)
                        K.tt("pool", t1[:], t1[:], t2[:], ALU.add)
                        K.tt("dve", yb[:], PS[2][:, :], t1[:], ALU.add)
                        K.tt("dve", yb[:], yb[:], sz_t[:], ALU.mult)
                        K.memset("pool", ssy[:], 0.0)
                        K.act(junk5[:], yb[:], AF.Square, accum=ssy[:, 0:1])
                        K.ts("dve", ssy[:], ssy[:], 1.0 / 1024, EPS, op0=ALU.mult, op1=ALU.add)
                        K.act(ssy[:], ssy[:], AF.Sqrt)
                        K.recip(ssy[:], ssy[:])
                        K.ts("dve", yb[:], yb[:], ssy[:, 0:1], None, op0=ALU.mult)
                        if seq == 0 and c == 1:
                            dump("yb", yb[:], [128, 1024])

                    def p5_B2(c):
                        for half in range(2):
                            pb = bank(1, 1) if half == 0 else bank(1, 0)
                            for j in range(4):
                                K.tr(pb[:, j * 128:(j + 1) * 128], yb[:, (half * 4 + j) * 128:(half * 4 + j + 1) * 128], ident[:])
                            K.copy("act", ymT[:, half * 4:half * 4 + 4, :], pb.re("p (a b) -> p a b", a=4))
                        for half in range(2):
                            pb = bank(3, half)
                            for fc in range(12):
                                lhs = ymT[:, fc, :] if fc < 8 else yaT[:, fc - 8, (c - 1) * 128:c * 128]
                                K.mm(pb, lhs, wo[:, fc, half * 512:(half + 1) * 512], start=(fc == 0), stop=(fc == 11))
                        load_x(seq, c, xb[0])
                        ln_tile(xb[0][:], hres[:], 0, 1)
                        K.stt(rr[:], hres[:], ALPHA, PS[3][:, :], ALU.mult, ALU.add)
                        if seq == 0 and c == 1:
                            dump("rr", rr[:], [128, 1024])
                        ho = h1b[c % 2]
                        ln_tile(rr[:], ho[:], 2, 3)
                        K.dma("sp", h1_s[seq, (c - 1) * 128:c * 128, :], ho[:], writes=[h1_r.s(seq * NT + c - 1)], sem="h1w" + str(c % 2))

                    def p5_S(c):
                        xs_t = xsb[c % 2]
                        K.tt("dve", r3(xwf[:]), r3(xs_t[:]), WX[:, c, 0:16].bc(2, [128, 16, 64]), ALU.mult)
                        for g in range(2):
                            K.mm(bank(3, g), Btok[:, c, g, :], xwf[:, g * 512:(g + 1) * 512])
                        K.tt("dve", r3(tmpS[:]), r3(Sf[:]), CD[:, c, 0:16].bc(2, [128, 16, 64]), ALU.mult)
                        K.tt("dve", Sf[:], tmpS[:], PS[3][:, :], ALU.add)
                        K.copy("act", Sfb[:], Sf[:])

                    didx = [0]
                    p5_loads(0)
                    p5_loads(1)
                    p5_A(1)
                    for c in range(NTT):
                        if c >= 1:
                            p5_B1(c)
                        if c < NTT - 1:
                            p5_S(c)
                        if c + 2 < NTT:
                            p5_loads(c + 2)
                        if c >= 1 and c + 1 < NTT:
                            p5_A(c + 1)
                        if c >= 1:
                            p5_B2(c)
                    K.flush("p5_ssd")
                if seq == 0:
                    issue_casts(64)
                    K.flush("castfin")
                    cst_box[0] = None
                s3.close()

        NEG = -1.0e30
        with ExitStack() as s6:
            iota = K.sb("iota", [128, 128], F32, stack=s6)
            K.dma("sp", iota[:], iota_d)
            iotab = K.sb("iotab", [128, 128], BF16, stack=s6)
            K.copy("dve", iotab[:], iota[:])
            wq = K.sb("wq", [128, 8, 2048], BF16, stack=s6)
            K.dma("pool", wq[:], wq_d.rearrange("(kc p) c -> p kc c", p=128))
            skT = K.sb("skT", [128, 16, 128], F32, stack=s6)
            K.dma("sp", skT[:], skT_d)
            vecs6 = K.sb("vecs6", [128, 2, D], F32, stack=s6)
            for i in range(2):
                K.dma("sp", vecs6[:, i, :], vecs_d[4 + i:5 + i, :].to_broadcast([128, D]))
            vecs_box[0] = vecs6
            h1s = [K.sb("h1_0", [128, 2, D], F32, stack=s6)] * 2
            h1Ts = [K.sb(f"h1T_{i}", [128, 8, 256], BF16, stack=s6) for i in range(2)]
            ITss = [K.sb(f"ITs_{i}", [128, 2, 3, 128], F32, stack=s6) for i in range(2)]
            GTb = K.sb("GTb", [128, 256, 128], BF16, stack=s6)
            tiles = [(sq, t) for sq in range(NSEQ) for t in range(NT)]
            groups = [tiles[gi:gi + 2] for gi in range(0, len(tiles), 2)]
            NGRP = len(groups)

            def alloc_route(stack):
                R = {}
                R["qT"] = K.sb("qT", [128, 16, 128], F32, stack=stack)
                R["sc"] = [K.sb("sc0", [128, 16, 128], F32, nreg=16, stack=stack)] * 2
                R["V1"] = K.sb("V1", [128, 16, 16], F32, nreg=32, stack=stack)
                R["I1"] = K.sb("I1", [128, 16, 16], U32, nreg=32, stack=stack)
                R["I1f"] = K.sb("I1f", [128, 16, 16], F32, stack=stack)
                R["cand"] = K.sb("cand", [128, 8, 256], F32, nreg=8, stack=stack)
                R["TS"] = K.sb("TS", [128, 8, 16], F32, nreg=16, stack=stack)
                R["SEL"] = K.sb("SEL", [128, 8, 16], U32, nreg=16, stack=stack)
                R["J1"] = K.sb("J1", [128, 8, 16], U32, stack=stack)
                R["J2"] = K.sb("J2", [128, 8, 16], U32, stack=stack)
                R["J1f"] = K.sb("J1f", [128, 8, 16], F32, stack=stack)
                R["J2f"] = K.sb("J2f", [128, 8, 16], F32, stack=stack)
                R["EG"] = K.sb("EG", [128, 3, 128], F32, stack=stack)
                R["gs"] = K.sb("gs", [128, 8], F32, stack=stack)
                return R

            def topk16_gen(n, src, vals, idxs):
                for f in range(n):
                    K.op("dve", lambda e, f=f: e.max(out=vals.h[:, f, 0:8], in_=src.h[:, f, :]), reads=[src.s(f)], writes=[vals.s(2 * f)])
                    if f % 4 == 3:
                        yield
                for f in range(n):
                    K.op("dve", lambda e, f=f: e.max_index(out=idxs.h[:, f, 0:8], in_max=vals.h[:, f, 0:8], in_values=src.h[:, f, :]),
                         reads=[src.s(f), vals.s(2 * f)], writes=[idxs.s(2 * f)])
                    if f % 4 == 3:
                        yield
                for f in range(n):
                    K.op("dve", lambda e, f=f: e.match_replace(out=src.h[:, f, :], in_to_replace=vals.h[:, f, 0:8], in_values=src.h[:, f, :], imm_value=NEG),
                         reads=[vals.s(2 * f)], writes=[src.s(f)])
                    if f % 4 == 3:
                        yield
                for f in range(n):
                    K.op("dve", lambda e, f=f: e.max(out=vals.h[:, f, 8:16], in_=src.h[:, f, :]), reads=[src.s(f)], writes=[vals.s(2 * f + 1)])
                    if f % 4 == 3:
                        yield
                for f in range(n):
                    K.op("dve", lambda e, f=f: e.max_index(out=idxs.h[:, f, 8:16], in_max=vals.h[:, f, 8:16], in_values=src.h[:, f, :]),
                         reads=[src.s(f), vals.s(2 * f + 1)], writes=[idxs.s(2 * f + 1)])
                    if f % 4 == 3:
                        yield

            IDLE = 24

            def routing_gen(g, R):
                par = g % 2
                grp = groups[g]
                h1 = h1s[par]
                h1T = h1Ts[par]
                ITs = ITss[par]
                qT, scs, V1, I1, I1f, cand, TS, SEL = R["qT"], R["sc"], R["V1"], R["I1"], R["I1f"], R["cand"], R["TS"], R["SEL"]
                J1, J2, J1f, J2f, EG, gs = R["J1"], R["J2"], R["J1f"], R["J2f"], R["EG"], R["gs"]
                qT3 = qT[:]
                oh4 = cand[:].re("p h (k j) -> p h k j", k=16)
                for j, (sq, t) in enumerate(grp):
                    K.dma("sp", h1[:, j, :], h1_s[sq, t * 128:(t + 1) * 128, :], reads=[h1_r.s(sq * NT + t)], sem="h1l" + str(par))
                    for half in range(2):
                        pb = bank(1, half)
                        for q in range(4):
                            kc = half * 4 + q
                            K.tr(pb[:, q * 128:(q + 1) * 128], h1[:, j, kc * 128:(kc + 1) * 128], ident[:])
                        K.copy("act", h1T[:, half * 4:half * 4 + 4, j * 128:(j + 1) * 128], pb.re("p (a b) -> p a b", a=4))
                        yield
                for j in range(len(grp)):
                    tk = slice(j * 128, (j + 1) * 128)
                    sc = scs[j % 2]
                    for f in range(16):
                        pb = bank(1, f % 2, 128)
                        for kc in range(8):
                            K.mm(pb, wq[:, kc, f * 128:(f + 1) * 128], h1T[:, kc, tk], start=(kc == 0), stop=(kc == 7))
                        K.copy("act", qT3[:, f, :], pb)
                        yield
                    if j >= 1:
                        for _ in range(IDLE):
                            yield
                    for hh in range(2):
                        for f8 in range(8):
                            f = hh * 8 + f8
                            K.mm(PS[1].s(f8 // 4, (slice(None), slice(f8 * 128, (f8 + 1) * 128))), qT3[:, f, :], skT[:, f, :])
                        K.copy("act", sc.ss(range(hh * 8, hh * 8 + 8), (slice(None), slice(hh * 8, hh * 8 + 8), slice(None))),
                               PS[1][:, :].re("p (a b) -> p a b", a=8))
                        yield
                    yield from topk16_gen(16, sc, V1, I1)
                    K.copy("dve", I1f[:], I1[:])
                    V4 = V1[:].re("p (h a) j -> p h a j", a=2)
                    I4 = I1f[:].re("p (h a) j -> p h a j", a=2)
                    c4 = cand[:].re("p h (a b) -> p h a b", a=16)
                    K.tt("dve", c4, V4[:, :, 0, :].bc(3, [128, 8, 16, 16]), V4[:, :, 1, :].bc(2, [128, 8, 16, 16]), ALU.add)
                    yield
                    yield from topk16_gen(8, cand, TS, SEL)
                    G3 = EG[:, 2, :].re("p (h k) -> p h k", h=8)
                    K.tt("dve", G3, TS[:], TS[:, :, 0].bc(2, [128, 8, 16]), ALU.subtract)
                    for _ in range(IDLE):
                        yield
                    K.act(G3, G3, AF.Exp)
                    K.reduce(gs[:], G3, ALU.add)
                    K.recip(gs[:], gs[:])
                    K.tt("dve", G3, G3, gs[:].bc(2, [128, 8, 16]), ALU.mult)
                    yield
                    K.op("dve", lambda e: e.tensor_single_scalar(out=J1.h[:], in_=SEL.h[:], scalar=4, op=ALU.logical_shift_right), reads=[SEL[:]], writes=[J1[:]])
                    K.op("dve", lambda e: e.tensor_single_scalar(out=J2.h[:], in_=SEL.h[:], scalar=15, op=ALU.bitwise_and), reads=[SEL[:]], writes=[J2[:]])
                    K.copy("dve", J1f[:], J1[:])
                    K.copy("dve", J2f[:], J2[:])
                    yield
                    for which, (Jf, half) in enumerate(((J1f, 0), (J2f, 1))):
                        K.tt("dve", oh4, Jf[:].bc(3, [128, 8, 16, 16]),
                             V(iota.h[:, 0:16].unsqueeze(1).unsqueeze(1).to_broadcast([128, 8, 16, 16]), iota.regs), ALU.is_equal)
                        K.tt("dve", oh4, oh4, I4[:, :, half, :].bc(2, [128, 8, 16, 16]), ALU.mult)
                        K.reduce(EG[:, which, :].re("p (h k) -> p h k", h=8), oh4, ALU.add)
                        yield
                    for _ in range(IDLE):
                        yield
                    for w3 in range(3):
                        K.tr(bank(1, 0)[:, w3 * 128:(w3 + 1) * 128], EG[:, w3, :], ident[:])
                    K.copy("act", ITs[:, j, :, :], bank(1, 0)[:, 0:384].re("p (a b) -> p a b", a=3))
                    if g == 0 and j == 0:
                        dump("EG", EG[:], [128, 3, 128])
                    yield

            with ExitStack() as sr0:
                R0 = alloc_route(sr0)
                for _ in routing_gen(0, R0):
                    pass
                K.flush("r_route")

            for g in range(NGRP):
                grp = groups[g]
                par = g % 2
                h1 = h1s[par]
                h1T = h1Ts[par]
                ITs = ITss[par]
                with ExitStack() as sg:
                    A1s = [K.sb(f"A1_{i}", [128, 32, 128], BF16, stack=sg) for i in range(2)]
                    A2s = [K.sb(f"A2_{i}", [128, 32, 128], BF16, nreg=32, stack=sg) for i in range(2)]
                    pidx = 0
                    hidx = 0
                    for j in range(len(grp)):
                        for hf in range(4):
                            A1 = A1s[hidx % 2]
                            A2 = A2s[hidx % 2]
                            hidx += 1
                            ts_ = slice(hf * 32, (hf + 1) * 32)
                            io3 = V(iota.h[:].unsqueeze(1).to_broadcast([128, 32, 128]), iota.regs)
                            K.tt("dve", A1[:], io3, ITs[:, j, 0, ts_].bc(2, [128, 32, 128]), ALU.is_equal)
                            for tq in range(32):
                                tcol = hf * 32 + tq
                                K.ts("dve", A2.s(tq, (slice(None), tq, slice(None))), iotab[:], ITs[:, j, 1, tcol:tcol + 1], ITs[:, j, 2, tcol:tcol + 1],
                                     op0=ALU.is_equal, op1=ALU.mult)
                            for t4 in range(8):
                                pb = bank(pidx % 4, (pidx // 4) % 2)
                                pidx += 1
                                for q in range(4):
                                    tt_ = t4 * 4 + q
                                    K.mm(pb[:, q * 128:(q + 1) * 128], A2.s(tt_, (slice(None), tt_, slice(None))), A1[:, tt_, :])
                                t0 = j * 128 + hf * 32 + t4 * 4
                                K.copy("act", GTb[:, t0:t0 + 4, :], pb.re("p (t i) -> p t i", t=4))
                    K.flush("g_build")
                with ExitStack() as sd:
                    ub = [K.sb(f"ub{i}", [128, 2, 1024], BF16, stack=sd) for i in range(3)]
                    vb = [K.sb(f"vb{i}", [128, 2, 1024], BF16, stack=sd) for i in range(3)]
                    ga = [K.sb(f"ga{i}", [128, 256], BF16, stack=sd) for i in range(4)]
                    cT = [K.sb(f"cT{i}", [128, 256], BF16, stack=sd) for i in range(4)]
                    r6 = K.sb("r6", [128, D], F32, stack=sd)
                    gen = None
                    if g + 1 < NGRP:
                        Rn = alloc_route(sd)
                        gen = routing_gen(g + 1, Rn)
                    NTK = len(grp) * 128
                    LAGD = 2
                    pend = []

                    def emit_v(ec, c_, v_t, jj):
                        for j in range(len(grp)):
                            for half in range(2):
                                K.mm(bank(2 + j, half), c_[:, j * 128:(j + 1) * 128], v_t[:, jj, half * 512:(half + 1) * 512],
                                     start=(ec == 0), stop=(ec == 127))

                    for eb in range(64):
                        u_t = ub[eb % 3]
                        v_t = vb[eb % 3]
                        K.dma("sp", u_t[:], uT_s[eb * 256:(eb + 1) * 256, :].rearrange("(j p) c -> p j c", p=128), reads=[uv_r.s(0)], sem="ul" + str(eb % 3))
                        K.dma("sp", v_t[:], v_s[eb * 256:(eb + 1) * 256, :].rearrange("(j p) c -> p j c", p=128), reads=[uv_r.s(1)], sem="vl" + str(eb % 3))
                        for jj in range(2):
                            ec = eb * 2 + jj
                            pa = bank(0, ec % 2, NTK)
                            for kc in range(8):
                                K.mm(pa, u_t[:, jj, kc * 128:(kc + 1) * 128], h1T[:, kc, 0:NTK], start=(kc == 0), stop=(kc == 7))
                            g_ = ga[ec % 4]
                            c_ = cT[ec % 4]
                            K.act(g_[:, 0:NTK], pa, AF.Gelu)
                            K.tt("pool", c_[:, 0:NTK], g_[:, 0:NTK], GTb[:, 0:NTK, ec], ALU.mult)
                            pend.append((ec, c_, v_t, jj))
                            if len(pend) > LAGD:
                                emit_v(*pend.pop(0))
                            if gen is not None:
                                for _ in range(3):
                                    if next(gen, "done") == "done":
                                        gen = None
                                        break
                    while pend:
                        emit_v(*pend.pop(0))
                    if gen is not None:
                        for _ in gen:
                            pass
                    for j, (sq, t) in enumerate(grp):
                        K.dma("sp", h1[:, j, :], h1_s[sq, t * 128:(t + 1) * 128, :], reads=[h1_r.s(sq * NT + t)], sem="h1l" + str(par))
                    for j, (sq, t) in enumerate(grp):
                        K.stt(r6[:], h1[:, j, :], ALPHA, PS[2 + j][:, :], ALU.mult, ALU.add)
                        if g == 0 and j == 0:
                            dump("r6", r6[:], [128, D])
                        ln_tile(r6[:], r6[:], 0, 1)
                        K.dma("sp", y_d[sq, t * 128:(t + 1) * 128, :], r6[:], sem="yw")
                    K.flush("d_dense")
    return nc, dbg_out


_CACHE = {}


def _host_consts(S):
    LP = S + 128
    pos = np.arange(LP, dtype=np.float32) - 112.0
    inv = (10000.0 ** (-np.arange(16, dtype=np.float32) / 16)).astype(np.float32)
    ang = pos[:, None] * inv[None, :]
    cossin = np.concatenate([np.cos(ang), np.sin(ang)], 1).astype(np.float32)
    ar = np.arange(128)
    return dict(
        cossin=cossin,
        ident=np.eye(128, dtype=np.float32),
        padmask=(ar >= 112).astype(np.float32).reshape(128, 1),
        mask_le=(ar[:, None] <= ar[None, :]).astype(np.float32),
        mask_ge=(ar[:, None] >= ar[None, :]).astype(np.float32),
        iota=np.ascontiguousarray(np.broadcast_to(np.arange(128, dtype=np.float32), (128, 128))),
    )


def _prep_weights(inp):
    f = lambda a: np.ascontiguousarray(np.asarray(a, dtype=np.float32))
    vecs = np.zeros((8, 1024), np.float32)
    for i, k in enumerate(["ln_in_g", "ln_in_b", "ln1_g", "ln1_b", "ln2_g", "ln2_b"]):
        vecs[i] = np.asarray(inp[k]).reshape(-1)
    return dict(
        meta=f(inp["meta_tokens"]), vecs=vecs, w_in=f(inp["w_in"][0]), w_uq=f(inp["w_uq"][0]), w_ukv=f(inp["w_ukv"][0]),
        q_norm_g=f(np.asarray(inp["q_norm_g"][0]).reshape(3, 128).T), kv_norm_g=f(np.asarray(inp["kv_norm_g"][0]).reshape(2, 128).T),
        conv_w=f(np.asarray(inp["conv_w"][0]).T.reshape(12, 128, 5).transpose(1, 0, 2)),
        conv_b=f(np.asarray(inp["conv_b"][0]).reshape(12, 128).T),
        dt_bias=f(np.concatenate([inp["dt_bias_fwd"][0], inp["dt_bias_bwd"][0]]).reshape(1, 32)),
        a_log=f(np.concatenate([inp["a_log_fwd"][0], inp["a_log_bwd"][0]]).reshape(1, 32)),
        d_skip=f(np.asarray(inp["d_skip"][0]).reshape(1, 16)),
        w_out=f(inp["w_out"][0]),
        norm_g=f(np.concatenate([inp["ssm_norm_g"][0], inp["attn_norm_g"][0]]).reshape(12, 128).T),
        peer_wq=f(inp["peer_w_query"][0]),
        peer_skT=f(np.asarray(inp["peer_sub_keys"][0]).transpose(3, 1, 0, 2).reshape(128, 16, 128)),
        peer_uT=f(np.asarray(inp["peer_u"][0]).reshape(128, 128, 8, 128).transpose(0, 3, 2, 1).reshape(128 * 128, 1024)),
        peer_v=f(inp["peer_v"][0]),
    )


def kernel(**inp):
    xp = np.asarray(inp["x_prompt"], dtype=np.float32)
    xs = np.asarray(inp["x_sample"], dtype=np.float32)
    S = xp.shape[1]
    allx = np.concatenate([xp, xs], 0)
    ncores = 8
    nseq = allx.shape[0] // ncores
    key = (S, nseq)
    if key not in _CACHE:
        _CACHE[key] = build(S, nseq)[0]
    nc = _CACHE[key]
    base = _prep_weights(inp)
    base.update(_host_consts(S))
    in_maps = []
    for c in range(ncores):
        m = dict(base)
        m["x"] = np.ascontiguousarray(allx[c * nseq:(c + 1) * nseq])
        in_maps.append(m)
    res = run_bass_kernel_spmd(nc, in_maps, core_ids=list(range(ncores)))
    ys = np.concatenate([np.asarray(r["y"]) for r in res.results], 0)
    nb = xp.shape[0]
    return (np.ascontiguousarray(ys[:nb]), np.ascontiguousarray(ys[nb:]))
```
